# Optimizing a Trainium2 kernel written in Bass

```python
import jax, jax.numpy as jnp
from jax import lax
import numpy as np

D_MODEL = 1024
BATCH = 8
SEQ = 4096
DEPTH = 1

GLA_HEADS = 4
GLA_DK = 128
GLA_DV = 256
GLA_QK_W = GLA_HEADS * GLA_DK
GLA_V_W = GLA_HEADS * GLA_DV
GATE_RANK = 16
GATE_NORM = 16.0
CHUNK = 64
CONV_W = 1024
CONV_WIDTH = 3
MIX_W = GLA_V_W + CONV_W
SPLITS = [GLA_QK_W, GLA_QK_W, GLA_V_W, GLA_V_W, GATE_RANK, GATE_RANK,
          CONV_W, CONV_W, CONV_W, CONV_W]
IN_W = sum(SPLITS)
EPS = 1e-6

kernel_name = "hybrid_gla_shortconv_parallel_heads"


def rmsnorm(x, g):
    xf = x.astype(jnp.float32)
    y = xf * lax.rsqrt(jnp.mean(xf * xf, axis=-1, keepdims=True) + EPS)
    return (y * g.astype(jnp.float32)).astype(x.dtype)


def gla_direction(q, k, v, g, strict):
    bsz, nh, s, dk = q.shape
    dv = v.shape[-1]
    n = s // CHUNK
    q = q.reshape(bsz, nh, n, CHUNK, dk)
    k = k.reshape(bsz, nh, n, CHUNK, dk)
    v = v.reshape(bsz, nh, n, CHUNK, dv)
    g = g.reshape(bsz, nh, n, CHUNK, dk)
    b = jnp.cumsum(g, axis=3)
    b_ref = b[:, :, :, CHUNK // 2:CHUNK // 2 + 1, :]
    att = jnp.einsum('bhncd,bhnjd->bhncj', q * jnp.exp(b - b_ref), k * jnp.exp(b_ref - b))
    mask = jnp.tril(jnp.ones((CHUNK, CHUNK), dtype=bool), k=-1 if strict else 0)
    att = jnp.where(mask, att, 0.0)
    o_intra = jnp.einsum('bhncj,bhnjv->bhncv', att, v)
    b_last = b[:, :, :, -1:, :]
    q_in = q * jnp.exp(b)
    k_out = k * jnp.exp(b_last - b)
    decay_chunk = jnp.exp(b_last[:, :, :, 0, :])
    xs = (jnp.moveaxis(q_in, 2, 0), jnp.moveaxis(k_out, 2, 0),
          jnp.moveaxis(v, 2, 0), jnp.moveaxis(decay_chunk, 2, 0))

    def step(state, inp):
        qc, kc, vc, dc = inp
        o = jnp.einsum('bhcd,bhdv->bhcv', qc, state)
        state = dc[..., None] * state + jnp.einsum('bhcd,bhcv->bhdv', kc, vc)
        return state, o

    s0 = jnp.zeros((bsz, nh, dk, dv), jnp.float32)
    _, o_inter = lax.scan(step, s0, xs)
    o = o_intra + jnp.moveaxis(o_inter, 0, 2)
    return o.reshape(bsz, nh, s, dv)


def to_heads(t, d):
    bsz, s, _ = t.shape
    return t.reshape(bsz, s, -1, d).transpose(0, 2, 1, 3)


def hybrid_mixer(h, w_in, w_gk_f, b_gk_f, w_gk_b, b_gk_b, gla_norm_g, conv_w, conv_b, w_out):
    bsz, s, _ = h.shape
    proj = jnp.einsum('bsd,de->bse', h, w_in)
    idx = np.cumsum(SPLITS)[:-1].tolist()
    (q, k, v, z_a, lr_f, lr_b, b_gate, c_gate, h_c, z_c) = jnp.split(proj, idx, axis=-1)
    f32 = jnp.float32
    q = to_heads(q.astype(f32), GLA_DK) * (GLA_DK ** -0.5)
    k = to_heads(k.astype(f32), GLA_DK)
    v = to_heads(v.astype(f32), GLA_DV)
    g_f = jax.nn.log_sigmoid(jnp.einsum('bsr,re->bse', lr_f.astype(f32), w_gk_f.astype(f32))
                             + b_gk_f.astype(f32)) / GATE_NORM
    g_b = jax.nn.log_sigmoid(jnp.einsum('bsr,re->bse', lr_b.astype(f32), w_gk_b.astype(f32))
                             + b_gk_b.astype(f32)) / GATE_NORM
    g_f = to_heads(g_f, GLA_DK)
    g_b = to_heads(g_b, GLA_DK)
    o_fwd = gla_direction(q, k, v, g_f, strict=False)
    flip = lambda t: jnp.flip(t, axis=2)
    o_bwd = flip(gla_direction(flip(q), flip(k), flip(v), flip(g_b), strict=True))
    o = o_fwd + o_bwd
    o = o * lax.rsqrt(jnp.mean(o * o, axis=-1, keepdims=True) + EPS) * gla_norm_g.astype(f32)
    y_a = o.transpose(0, 2, 1, 3).reshape(bsz, s, GLA_V_W)
    y_a = (y_a * jax.nn.silu(z_a.astype(f32))).astype(h.dtype)
    u = c_gate * h_c
    up = jnp.pad(u, ((0, 0), (1, 1), (0, 0)))
    conv = (conv_w[0] * up[:, :-2] + conv_w[1] * up[:, 1:-1] + conv_w[2] * up[:, 2:]) + conv_b
    y_c = b_gate * conv * jax.nn.silu(z_c)
    y = jnp.concatenate([y_a, y_c.astype(h.dtype)], axis=-1)
    return jnp.einsum('bse,ed->bsd', y, w_out)


def setup_inputs(seed: int = 0) -> dict:
    key = jax.random.key(seed)
    ks = jax.random.split(key, 14)
    nrm = lambda k_, shp, sc: jax.random.normal(k_, shp, jnp.float32) * sc
    return {
        "x": nrm(ks[0], (BATCH, SEQ, D_MODEL), 1.0),
        "norm_g": 1.0 + nrm(ks[1], (DEPTH, D_MODEL), 0.02),
        "w_in": nrm(ks[2], (DEPTH, D_MODEL, IN_W), D_MODEL ** -0.5),
        "w_gk_f": nrm(ks[3], (DEPTH, GATE_RANK, GLA_QK_W), GATE_RANK ** -0.5),
        "b_gk_f": nrm(ks[4], (DEPTH, GLA_QK_W), 0.1),
        "w_gk_b": nrm(ks[5], (DEPTH, GATE_RANK, GLA_QK_W), GATE_RANK ** -0.5),
        "b_gk_b": nrm(ks[6], (DEPTH, GLA_QK_W), 0.1),
        "gla_norm_g": 1.0 + nrm(ks[7], (DEPTH, GLA_DV), 0.02),
        "conv_w": nrm(ks[8], (DEPTH, CONV_WIDTH, CONV_W), CONV_WIDTH ** -0.5),
        "conv_b": nrm(ks[9], (DEPTH, CONV_W), 0.02),
        "w_out": nrm(ks[10], (DEPTH, MIX_W, D_MODEL), MIX_W ** -0.5),
        "final_g": 1.0 + nrm(ks[11], (D_MODEL,), 0.02),
    }


def reference(x, norm_g, w_in, w_gk_f, b_gk_f, w_gk_b, b_gk_b, gla_norm_g, conv_w, conv_b, w_out, final_g):
    for layer in range(DEPTH):
        h = rmsnorm(x, norm_g[layer])
        x = x + hybrid_mixer(h, w_in[layer], w_gk_f[layer], b_gk_f[layer], w_gk_b[layer],
                             b_gk_b[layer], gla_norm_g[layer], conv_w[layer], conv_b[layer],
                             w_out[layer])
    return rmsnorm(x, final_g)
```

```python
import numpy as np
from contextlib import ExitStack
import concourse.bass as bass
import concourse.mybir as mybir
from concourse.bass_utils import run_bass_kernel_spmd

F32 = mybir.dt.float32
BF16 = mybir.dt.bfloat16
AF = mybir.ActivationFunctionType
ALU = mybir.AluOpType

D = 1024
KC = 8
IN_W = 7200
EPS = 1e-6
OQ, OK_, OV, OZA, OLR, OB, OC, OH, OZC = 0, 512, 1024, 2048, 3072, 3104, 4128, 5152, 6176


class T:
    __slots__ = ("name", "w", "r", "dsem", "dcount")

    def __init__(self, name):
        self.name = name
        self.w = None
        self.r = {}
        self.dsem = None
        self.dcount = None


class EngQ:
    def __init__(self, name):
        self.name = name
        self.ops = []
        self.sem = None
        self.count = 0
        self.waited = {}


class Sched:
    ENGS = ("pe", "act", "dve", "pool", "sp")

    def __init__(self, nc, stack):
        self.nc = nc
        self.stack = stack
        self.q = {n: EngQ(n) for n in self.ENGS}
        for n in self.ENGS:
            self.q[n].sem = stack.enter_context(nc.semaphore("sem_" + n))
        self.nsem = 0

    def _need(self, q, ev, waits):
        if ev is None:
            return
        sem, val = ev
        k = id(sem)
        if sem is q.sem and val > q.count:
            return
        if q.waited.get(k, 0) >= val:
            return
        q.waited[k] = val
        waits[k] = (sem, max(val, waits.get(k, (None, 0))[1]))

    def _deps(self, q, reads, writes):
        waits = {}
        for t in reads:
            self._need(q, t.w, waits)
        for t in writes:
            self._need(q, t.w, waits)
            for ev in t.r.values():
                self._need(q, ev, waits)
        return list(waits.values())

    def _mark(self, ev, reads, writes):
        k = id(ev[0])
        for t in reads:
            old = t.r.get(k)
            if old is None or old[1] < ev[1]:
                t.r[k] = ev
        for t in writes:
            t.w = ev
            t.r = {}

    def op(self, eng, fn, reads=(), writes=(), signal=True):
        q = self.q[eng]
        waits = self._deps(q, reads, writes)
        ev = (q.sem, q.count + 1)
        if signal:
            q.count += 1
        sem = q.sem

        def emit(e, fn=fn, waits=waits, signal=signal, sem=sem):
            for (s, v) in waits:
                e.wait_ge(s, v)
            ins = fn(e)
            if signal:
                ins.then_inc(sem, 1)
        q.ops.append(emit)
        self._mark(ev, reads, writes)

    def dma(self, eng, out, in_, reads=(), writes=(), tsem=None, **kw):
        q = self.q[eng]
        waits = self._deps(q, reads, writes)
        t = tsem if tsem is not None else (writes[0] if writes else reads[0])
        if t.dsem is None:
            t.dsem = {}
            t.dcount = {}
        if eng not in t.dsem:
            t.dsem[eng] = self.stack.enter_context(
                self.nc.semaphore("dsem_%d" % self.nsem))
            t.dcount[eng] = 0
            self.nsem += 1
        t.dcount[eng] += 16
        dsem = t.dsem[eng]
        ev = (dsem, t.dcount[eng])

        def emit(e, waits=waits, out=out, in_=in_, sem=dsem, kw=kw):
            for (s, v) in waits:
                e.wait_ge(s, v)
            e.dma_start(out=out, in_=in_, **kw).then_inc(sem, 16)
        q.ops.append(emit)
        self._mark(ev, reads, writes)
        return ev

    def wait_all(self, eng, tiles):
        q = self.q[eng]
        waits = self._deps(q, (), tiles)

        def emit(e, waits=waits):
            for (s, v) in waits:
                e.wait_ge(s, v)
        q.ops.append(emit)

    def barrier(self):
        evs = [(self.q[n].sem, self.q[n].count) for n in self.ENGS if self.q[n].count > 0]
        for n in self.ENGS:
            q = self.q[n]
            waits = {}
            for ev in evs:
                self._need(q, ev, waits)
            wl = list(waits.values())

            def emit(e, wl=wl):
                for (s, v) in wl:
                    e.wait_ge(s, v)
            q.ops.append(emit)

    def emit(self):
        with self.nc.Block() as block:
            @block.tensor
            def _(e):
                for f in self.q["pe"].ops:
                    f(e)

            @block.scalar
            def _(e):
                for f in self.q["act"].ops:
                    f(e)

            @block.vector
            def _(e):
                for f in self.q["dve"].ops:
                    f(e)

            @block.gpsimd
            def _(e):
                for f in self.q["pool"].ops:
                    f(e)

            @block.sync
            def _(e):
                for f in self.q["sp"].ops:
                    f(e)


class Ring:
    def __init__(self, bufs):
        self.bufs = bufs
        self.i = 0

    def next(self):
        b = self.bufs[self.i % len(self.bufs)]
        self.i += 1
        return b


def build(S=4096):
    NT = S // 128
    NG = S // 512
    nc = bass.Bass("TRN2", target_bir_lowering=False, dynamic_dma_scratch_size=12288)
    x_d = nc.dram_tensor("x", [S, D], F32, kind="ExternalInput").ap()
    norm_g_d = nc.dram_tensor("norm_g", [D], F32, kind="ExternalInput").ap()
    w_in_d = nc.dram_tensor("w_in", [D, IN_W], F32, kind="ExternalInput").ap()
    w_gk_f_d = nc.dram_tensor("w_gk_f", [16, 512], F32, kind="ExternalInput").ap()
    b_gk_f_d = nc.dram_tensor("b_gk_f", [512], F32, kind="ExternalInput").ap()
    w_gk_b_d = nc.dram_tensor("w_gk_b", [16, 512], F32, kind="ExternalInput").ap()
    b_gk_b_d = nc.dram_tensor("b_gk_b", [512], F32, kind="ExternalInput").ap()
    gng_d = nc.dram_tensor("gla_norm_g", [256], F32, kind="ExternalInput").ap()
    conv_w_d = nc.dram_tensor("conv_w", [3, D], F32, kind="ExternalInput").ap()
    conv_b_d = nc.dram_tensor("conv_b", [D], F32, kind="ExternalInput").ap()
    w_out_d = nc.dram_tensor("w_out", [2048, D], F32, kind="ExternalInput").ap()
    final_g_d = nc.dram_tensor("final_g", [D], F32, kind="ExternalInput").ap()
    out_d = nc.dram_tensor("out", [S, D], F32, kind="ExternalOutput").ap()
    yT_d = nc.dram_tensor("yT_scratch", [16, 128, S], BF16, kind="Internal").ap()
    wob_d = nc.dram_tensor("wout_bf16", [2048, D], BF16, kind="Internal").ap()

    w_in_v = w_in_d.rearrange("(k p) c -> p k c", p=128)
    w_out_v = w_out_d.rearrange("(c p) d -> p c d", p=128)
    yT_v = yT_d.rearrange("c p s -> p c s")
    wob_v = wob_d.rearrange("(c p) d -> p c d", p=128)

    with ExitStack() as st:
        sc = Sched(nc, st)

        def sb(name, shape, dt, stack=st):
            return stack.enter_context(nc.sbuf_tensor(name, shape, dt))

        pb = []
        pbt = []
        for i in range(8):
            pb.append(st.enter_context(nc.psum_tensor("pb%d" % i, [128, 512], F32)))
            pbt.append(T("pb%d" % i))
        pbh = [p.bitcast(BF16) for p in pb]

        hT = sb("hT", [128, KC, S], BF16)
        t_hT = [T("hT%d" % t) for t in range(NT)]
        ident = sb("ident", [128, 128], BF16); t_ident = T("ident")
        mask = sb("mask", [128, 256], F32); t_mask = T("mask")
        rm_f = sb("rm_f", [128, 512], F32); t_rm = T("rm")
        rm_b = sb("rm_b", [128, 512], F32)
        gcol = sb("gcol", [128, KC], F32); t_gcol = T("gcol")
        gexp = sb("gexp", [128, KC, 128], F32); t_gexp = T("gexp")
        gng = sb("gng", [128, 256], F32); t_gng = T("gng")
        fg = sb("fg", [128, D], F32); t_fg = T("fg")
        nbias = sb("nbias", [128, 8], F32); t_nbias = T("nbias")
        cw = sb("cw", [128, 3, KC], F32); t_cw = T("cw")
        cb = sb("cb", [128, KC], F32); t_cb = T("cb")
        wg = sb("wg", [32, 2, 512], BF16); t_wg = T("wg")
        lrT = sb("lrT", [32, S], BF16); t_lrT = [T("lrT%d" % g) for g in range(NG)]
        wlr = sb("wlr", [128, KC, 32], BF16); t_wlr = T("wlr")
        junk = sb("junk", [128, D], BF16); t_junk = T("junk")
        dummy = sb("dummy_t", [128, 8], F32); t_dummy = T("dummy")

        sc.op("pool", lambda e: e.memset(ident[:], 0.0), writes=[t_ident])
        sc.op("pool", lambda e: e.affine_select(out=ident[:], in_=ident[:], compare_op=ALU.not_equal,
                                                fill=1.0, base=0, pattern=[[-1, 128]], channel_multiplier=1),
              reads=[t_ident], writes=[t_ident])
        sc.op("pool", lambda e: e.memset(mask[:], 1.0), writes=[t_mask])
        sc.op("pool", lambda e: e.affine_select(out=mask[:, 0:128], in_=mask[:, 0:128], compare_op=ALU.is_ge,
                                                fill=0.0, base=0, pattern=[[1, 128]], channel_multiplier=-1),
              reads=[t_mask], writes=[t_mask])
        sc.op("pool", lambda e: e.affine_select(out=mask[:, 128:256], in_=mask[:, 128:256], compare_op=ALU.is_gt,
                                                fill=0.0, base=0, pattern=[[-1, 128]], channel_multiplier=1),
              reads=[t_mask], writes=[t_mask])
        sc.op("pool", lambda e: e.memset(rm_f[:], 1.0), writes=[t_rm])
        sc.op("pool", lambda e: e.memset(rm_b[:], 1.0), writes=[t_rm])
        sc.op("pool", lambda e: e.memset(rm_f[:, 0:512:128], 0.0), writes=[t_rm])
        sc.op("pool", lambda e: e.memset(rm_b[:, 127:512:128], 0.0), writes=[t_rm])
        sc.op("pool", lambda e: e.memset(wg[:], 0.0), writes=[t_wg])
        sc.op("pool", lambda e: e.memset(gexp[:], 1.0), writes=[t_gexp])

        sc.dma("sp", gcol[:], norm_g_d.rearrange("(c p) -> p c", p=128), writes=[t_gcol],
               allow_slow_non_contiguous=True)
        for c in range(KC):
            sc.op("dve", lambda e, c=c: e.tensor_scalar(out=gexp[:, c, :], in0=gexp[:, c, :],
                                                        scalar1=gcol[:, c:c + 1], scalar2=None, op0=ALU.mult),
                  reads=[t_gcol, t_gexp], writes=[t_gexp])

        def load_small_consts():
            sc.dma("pool", wg[0:16, 0, :], w_gk_f_d, writes=[t_wg])
            sc.dma("pool", wg[16:32, 1, :], w_gk_b_d, writes=[t_wg])
            sc.dma("pool", wlr[:], w_in_v[:, :, OLR:OLR + 32], writes=[t_wlr])
            sc.dma("pool", gng[:], gng_d.partition_broadcast(128), writes=[t_gng])
            sc.dma("pool", fg[:], final_g_d.partition_broadcast(128), writes=[t_fg])
            sc.dma("pool", nbias[:, 0:4], b_gk_f_d.rearrange("(h p) -> p h", p=128), writes=[t_nbias],
                   allow_slow_non_contiguous=True)
            sc.dma("pool", nbias[:, 4:8], b_gk_b_d.rearrange("(h p) -> p h", p=128), writes=[t_nbias],
                   allow_slow_non_contiguous=True)
            sc.dma("pool", cw[:], conv_w_d.rearrange("k (c p) -> p k c", p=128), writes=[t_cw],
                   allow_slow_non_contiguous=True)
            sc.dma("pool", cb[:], conv_b_d.rearrange("(c p) -> p c", p=128), writes=[t_cb],
                   allow_slow_non_contiguous=True)

        stat = sb("stat", [128, 64], F32)
        t_stat = [T("stat%d" % i) for i in range(16)]
        stat_i = [0]

        def new_stat():
            i = stat_i[0] % 16
            stat_i[0] += 1
            return (stat[:, 4 * i:4 * i + 1], stat[:, 4 * i + 1:4 * i + 2], stat[:, 4 * i + 2:4 * i + 3], t_stat[i])

        def rms_rstd(src_ap, src_ts, n_elems, junk_ap):
            ss, lnv, rstd, ts = new_stat()
            sc.op("act", lambda e: e.activation(out=junk_ap, in_=src_ap, func=AF.Square, accum_out=ss),
                  reads=src_ts, writes=[t_junk, ts])
            sc.op("act", lambda e: e.activation(out=lnv, in_=ss, func=AF.Ln, bias=EPS, scale=1.0 / n_elems),
                  reads=[ts], writes=[ts])
            sc.op("act", lambda e: e.activation(out=rstd, in_=lnv, func=AF.Exp, scale=-0.5),
                  reads=[ts], writes=[ts])
            return rstd, ts

        wq = sb("wq", [128, KC, 128], BF16); t_wq = T("wq")
        wk = sb("wk", [128, KC, 128], BF16); t_wk = T("wk")
        wv = sb("wv", [128, KC, 256], BF16); t_wv = T("wv")
        wz = sb("wz", [128, KC, 256], BF16); t_wz = T("wz")
        p12 = ExitStack()
        p12.__enter__()
        p1 = p12
        xr = Ring([(sb("xt%d" % i, [128, D], F32, p1), T("xt%d" % i)) for i in range(6)])
        xnr = Ring([(sb("xn%d" % i, [128, D], BF16, p1), T("xn%d" % i)) for i in range(3)])

        def p1_tile(t):
            xt, t_xt = xr.next()
            xn, t_xn = xnr.next()
            bank = t % 4
            sc.dma("sp", xt[:], x_d[t * 128:(t + 1) * 128, :], writes=[t_xt])
            rstd, ts = rms_rstd(xt[:], [t_xt], D, junk[:])
            XS = 384
            sc.op("act", lambda e: e.activation(out=xn[:, 0:XS], in_=xt[:, 0:XS], func=AF.Copy, scale=rstd),
                  reads=[t_xt, ts], writes=[t_xn])
            sc.op("dve", lambda e: e.tensor_scalar(out=xn[:, XS:D], in0=xt[:, XS:D], scalar1=rstd, scalar2=None, op0=ALU.mult),
                  reads=[t_xt, ts], writes=[t_xn])
            for c in range(KC):
                sc.op("pe", lambda e, c=c: e.transpose(
                    pbh[bank][:, c * 128:(c + 1) * 128], xn[:, c * 128:(c + 1) * 128], ident[:]),
                    reads=[t_xn, t_ident], writes=[pbt[bank]], signal=(c == KC - 1))
            sc.op("dve", lambda e: e.tensor_tensor(
                out=hT[:, :, t * 128:(t + 1) * 128],
                in0=pbh[bank][:, :].rearrange("p (c t) -> p c t", c=KC),
                in1=gexp[:], op=ALU.mult),
                reads=[pbt[bank], t_gexp], writes=[t_hT[t]])


        def load_qkv(h):
            sc.dma("pool", wq[:], w_in_v[:, :, OQ + h * 128:OQ + (h + 1) * 128], writes=[t_wq])
            sc.dma("pool", wk[:], w_in_v[:, :, OK_ + h * 128:OK_ + (h + 1) * 128], writes=[t_wk])
            sc.dma("pool", wv[:], w_in_v[:, :, OV + h * 256:OV + (h + 1) * 256], writes=[t_wv])

        def load_z(h):
            sc.dma("pool", wz[:], w_in_v[:, :, OZA + h * 256:OZA + (h + 1) * 256], writes=[t_wz])

        hT_g = [t_hT[4 * g:4 * g + 4] for g in range(NG)]

        if True:
            p2 = p12
            wcr = Ring([(sb("wc%d" % i, [128, KC, 4, 128], BF16, p2), [T("wc%d_%d" % (i, j)) for j in range(4)])
                        for i in range(2)])
            u_full = sb("u_full", [128, S + 2], F32, p2)
            t_u = [T("u%d" % g) for g in range(NG)]
            t_upad = T("upad")
            hcr = Ring([(sb("hcs%d" % i, [128, 512], F32, p2), T("hcs%d" % i)) for i in range(2)])
            sgr = Ring([(sb("sg%d" % i, [128, 512], F32, p2), T("sg%d" % i)) for i in range(2)])
            bzr = Ring([(sb("bz%d" % i, [128, 512], F32, p2), T("bz%d" % i)) for i in range(3)])
            t1r = Ring([(sb("t1_%d" % i, [128, 512], F32, p2), T("t1_%d" % i)) for i in range(2)])
            ycr = Ring([(sb("yc%d" % i, [128, S], BF16, p2), T("yc%d" % i)) for i in range(2)])
            sc.op("pool", lambda e: e.memset(u_full[:, 0:1], 0.0), writes=[t_upad])
            sc.op("pool", lambda e: e.memset(u_full[:, S + 1:S + 2], 0.0), writes=[t_upad])
            offs = [OB, OC, OH, OZC]
            t_wob = [T("wob%d" % i) for i in range(4)]

            def load_wc(c):
                wc, t_wc = wcr.next()
                for part in range(4):
                    sc.dma("pool", wc[:, :, part, :], w_in_v[:, :, offs[part] + c * 128: offs[part] + (c + 1) * 128],
                           writes=[t_wc[part]])
                return wc, t_wc

            wc_next = load_wc(0)
            load_small_consts()
            for c in range(KC):
                wc, t_wc = wc_next
                if c + 1 < KC:
                    wc_next = load_wc(c + 1)
                yc, t_yc = ycr.next()
                bz_of = {}

                def conv_group(g, c=c, yc=yc, t_yc=t_yc):
                    c0 = 1 + g * 512
                    t1, t_t1 = t1r.next()
                    bz, t_bz = bz_of.pop(g)
                    rd = [t_u[g], t_upad]
                    if g > 0:
                        rd.append(t_u[g - 1])
                    if g < NG - 1:
                        rd.append(t_u[g + 1])
                    sc.op("pool", lambda e: e.tensor_scalar(out=t1[:], in0=u_full[:, c0:c0 + 512],
                                                            scalar1=cw[:, 1, c:c + 1], scalar2=cb[:, c:c + 1],
                                                            op0=ALU.mult, op1=ALU.add),
                          reads=rd + [t_cw, t_cb], writes=[t_t1])
                    sc.op("dve", lambda e: e.scalar_tensor_tensor(out=t1[:], in0=u_full[:, c0 - 1:c0 + 511],
                                                                  scalar=cw[:, 0, c:c + 1], in1=t1[:],
                                                                  op0=ALU.mult, op1=ALU.add),
                          reads=rd + [t_cw, t_t1], writes=[t_t1])
                    sc.op("dve", lambda e: e.scalar_tensor_tensor(out=t1[:], in0=u_full[:, c0 + 1:c0 + 513],
                                                                  scalar=cw[:, 2, c:c + 1], in1=t1[:],
                                                                  op0=ALU.mult, op1=ALU.add),
                          reads=rd + [t_cw, t_t1], writes=[t_t1])
                    sc.op("pool", lambda e: e.tensor_tensor(out=yc[:, g * 512:(g + 1) * 512], in0=t1[:], in1=bz[:],
                                                            op=ALU.mult),
                          reads=[t_t1, t_bz], writes=[t_yc])

                if c == min(1, KC - 1):
                    sc.op("pool", lambda e: e.tensor_scalar(out=nbias[:], in0=nbias[:], scalar1=-1.0, scalar2=None,
                                                            op0=ALU.mult), reads=[t_nbias], writes=[t_nbias])
                if 1 <= c <= 4:
                    i = c - 1
                    sc.dma("pool", wob_d[512 * i:512 * (i + 1), :], w_out_d[512 * i:512 * (i + 1), :], writes=[t_wob[i]])
                if c == KC - 2:
                    load_qkv(0)
                if c == KC - 1:
                    load_z(0)
                    for g_ in range(NG):
                        for k in range(KC):
                            sc.op("pe", lambda e, k=k, g_=g_: e.matmul(pb[0][0:32, :], lhsT=wlr[:, k, :],
                                                                       rhs=hT[:, k, g_ * 512:(g_ + 1) * 512],
                                                                       start=(k == 0), stop=(k == KC - 1)),
                                  reads=[t_wlr] + hT_g[g_], writes=[pbt[0]], signal=(k == KC - 1))
                        sc.op("act", lambda e, g_=g_: e.activation(out=lrT[:, g_ * 512:(g_ + 1) * 512], in_=pb[0][0:32, :], func=AF.Copy),
                              reads=[pbt[0]], writes=[t_lrT[g_]])
                for g in range(NG):
                    if c == 0 and g == 0:
                        for t in range(0, 4):
                            p1_tile(t)
                    b0 = 4 if c == 0 else 4 * (g % 2)
                    for part in range(4):
                        for k in range(KC):
                            sc.op("pe", lambda e, part=part, k=k, b0=b0, g=g, wc=wc: e.matmul(
                                pb[b0 + part][:, :], lhsT=wc[:, k, part, :], rhs=hT[:, k, g * 512:(g + 1) * 512],
                                start=(k == 0), stop=(k == KC - 1)),
                                reads=[t_wc[part]] + hT_g[g], writes=[pbt[b0 + part]], signal=(k == KC - 1))
                        if c == 0 and g + 1 < NG:
                            p1_tile(4 * (g + 1) + part)
                    p_b, p_c, p_h, p_z = (pb[b0 + i] for i in range(4))
                    tb, tcg, th, tz = (pbt[b0 + i] for i in range(4))
                    sg, t_sg = sgr.next()
                    hcs, t_hcs = hcr.next()
                    bz, t_bz = bzr.next()
                    bz_of[g] = (bz, t_bz)
                    sc.op("act", lambda e, sg=sg, p_z=p_z: e.activation(out=sg[:], in_=p_z[:, :], func=AF.Exp, scale=-1.0),
                          reads=[tz], writes=[t_sg])
                    sc.op("act", lambda e, sg=sg: e.activation(out=sg[:], in_=sg[:], func=AF.Ln, bias=1.0, scale=1.0),
                          reads=[t_sg], writes=[t_sg])
                    sc.op("act", lambda e, sg=sg: e.activation(out=sg[:], in_=sg[:], func=AF.Exp, scale=-1.0),
                          reads=[t_sg], writes=[t_sg])
                    sc.op("act", lambda e, hcs=hcs, p_h=p_h: e.activation(out=hcs[:], in_=p_h[:, :], func=AF.Copy),
                          reads=[th], writes=[t_hcs])
                    sc.op("dve", lambda e, bz=bz, p_b=p_b, sg=sg: e.tensor_tensor(out=bz[:], in0=p_b[:, :], in1=sg[:], op=ALU.mult),
                          reads=[tb, t_sg], writes=[t_bz])
                    sc.op("dve", lambda e, bz=bz, p_z=p_z: e.tensor_tensor(out=bz[:], in0=p_z[:, :], in1=bz[:], op=ALU.mult),
                          reads=[tz, t_bz], writes=[t_bz])
                    sc.op("dve", lambda e, g=g, p_c=p_c, hcs=hcs: e.tensor_tensor(
                        out=u_full[:, 1 + g * 512:1 + (g + 1) * 512], in0=p_c[:, :], in1=hcs[:], op=ALU.mult),
                        reads=[tcg, t_hcs], writes=[t_u[g]])
                    if g > 0:
                        conv_group(g - 1)
                conv_group(NG - 1)
                sc.dma("sp", yT_v[:, 8 + c, :], yc[:], reads=[t_yc], tsem=t_yc)
            sc.wait_all("pool", [b[1] for b in ycr.bufs])
            sc.op("pool", lambda e: e.memset(dummy[:], 0.0), writes=[t_dummy])
            sc.barrier()
        p12.__exit__(None, None, None)

        with ExitStack() as p3:
            qf = sb("qf", [128, S], BF16, p3); qb = sb("qb", [128, S], BF16, p3)
            kf = sb("kf", [128, S], BF16, p3); kb = sb("kb", [128, S], BF16, p3)
            t_qk = [T("qk%d" % g) for g in range(NG)]
            v_h = sb("v_h", [128, NT, 256], BF16, p3)
            t_v = [T("v%d" % (i)) for i in range(NT // 2)]
            sbs = sb("sbs", [128, NT, 256], BF16, p3)
            t_sbs = [T("sbs%d" % i) for i in range(NT)]
            spf = sb("spf", [128, 512], F32, p3); t_spf = T("spf")
            spb = sb("spb", [128, 512], F32, p3); t_spb = T("spb")
            cumf = sb("cumf", [128, 512], F32, p3); t_cumf = T("cumf")
            cumb = sb("cumb", [128, 512], F32, p3); t_cumb = T("cumb")
            Ef = sb("Ef", [128, 512], F32, p3); t_Ef = T("Ef")
            Eb = sb("Eb", [128, 512], F32, p3); t_Eb = T("Eb")
            Eif = sb("Eif", [128, 512], F32, p3); t_Eif = T("Eif")
            Eib = sb("Eib", [128, 512], F32, p3); t_Eib = T("Eib")
            Dall = sb("Dall", [128, 2, NT], F32, p3); t_D = [T("D%d" % g) for g in range(NG)]
            s32r = Ring([(sb("S32_%d" % i, [128, 256], F32, p3), T("S32_%d" % i)) for i in range(3)])
            sfr = Ring([(sb("sfbf%d" % i, [128, 256], BF16, p3), T("sfbf%d" % i)) for i in range(4)])
            ktr = Ring([(sb("ktok%d" % i, [128, 128], BF16, p3), T("ktok%d" % i)) for i in range(6)])
            kor = Ring([(sb("kout%d" % i, [128, 128], BF16, p3), T("kout%d" % i)) for i in range(5)])
            atr = Ring([(sb("attm%d" % i, [128, 256], BF16, p3), T("attm%d" % i)) for i in range(3)])
            pbt7a = T("pb7a"); pbt7b = T("pb7b")
            sgr3 = Ring([(sb("sgz%d" % i, [128, 256], F32, p3), T("sgz%d" % i)) for i in range(2)])
            t1r3 = Ring([(sb("t1z%d" % i, [128, 256], F32, p3), T("t1z%d" % i)) for i in range(2)])
            t2r3 = Ring([(sb("t2z%d" % i, [128, 256], F32, p3), T("t2z%d" % i)) for i in range(2)])
            ytr = Ring([(sb("ytok%d" % i, [128, 256], BF16, p3), T("ytok%d" % i)) for i in range(4)])
            ysr = Ring([(sb("ystg%d" % i, [128, 2, 512], BF16, p3), T("ystg%d" % i)) for i in range(2)])
            qscale = 128.0 ** -0.5

            for h in range(4):
                for g in range(NG):
                    tok = slice(g * 512, (g + 1) * 512)
                    sc.op("pe", lambda e, h=h, tok=tok: e.matmul(pb[0][:, :], lhsT=wg[:, 0, h * 128:(h + 1) * 128],
                                                                 rhs=lrT[:, tok], start=True, stop=True),
                          reads=[t_wg, t_lrT[g]], writes=[pbt[0]])
                    sc.op("pe", lambda e, h=h, tok=tok: e.matmul(pb[1][:, :], lhsT=wg[:, 1, h * 128:(h + 1) * 128],
                                                                 rhs=lrT[:, tok], start=True, stop=True),
                          reads=[t_wg, t_lrT[g]], writes=[pbt[1]])
                    sc.op("act", lambda e, h=h: e.activation(out=spf[:], in_=pb[0][:, :], func=AF.Exp,
                                                             bias=nbias[:, h:h + 1], scale=-1.0),
                          reads=[pbt[0], t_nbias], writes=[t_spf])
                    sc.op("act", lambda e: e.activation(out=spf[:], in_=spf[:], func=AF.Ln, bias=1.0, scale=1.0),
                          reads=[t_spf], writes=[t_spf])
                    sc.op("act", lambda e, h=h: e.activation(out=spb[:], in_=pb[1][:, :], func=AF.Exp,
                                                             bias=nbias[:, 4 + h:5 + h], scale=-1.0),
                          reads=[pbt[1], t_nbias], writes=[t_spb])
                    sc.op("act", lambda e: e.activation(out=spb[:], in_=spb[:], func=AF.Ln, bias=1.0, scale=1.0),
                          reads=[t_spb], writes=[t_spb])
                    sc.op("dve", lambda e: e.tensor_tensor_scan(out=cumf[:], data0=rm_f[:], data1=spf[:], initial=0.0,
                                                                op0=ALU.mult, op1=ALU.add),
                          reads=[t_rm, t_spf], writes=[t_cumf])
                    sc.op("dve", lambda e: e.tensor_tensor_scan(out=cumb[:, ::-1], data0=rm_b[:, ::-1], data1=spb[:, ::-1],
                                                                initial=0.0, op0=ALU.mult, op1=ALU.add),
                          reads=[t_rm, t_spb], writes=[t_cumb])
                    sc.op("act", lambda e: e.activation(out=Ef[:], in_=cumf[:], func=AF.Exp, scale=-1.0 / 16),
                          reads=[t_cumf], writes=[t_Ef])
                    sc.op("act", lambda e: e.activation(out=Eif[:], in_=cumf[:], func=AF.Exp, scale=1.0 / 16),
                          reads=[t_cumf], writes=[t_Eif])
                    sc.op("act", lambda e: e.activation(out=Eb[:], in_=cumb[:], func=AF.Exp, scale=-1.0 / 16),
                          reads=[t_cumb], writes=[t_Eb])
                    sc.op("act", lambda e: e.activation(out=Eib[:], in_=cumb[:], func=AF.Exp, scale=1.0 / 16),
                          reads=[t_cumb], writes=[t_Eib])
                    sc.op("pool", lambda e, g=g: e.tensor_copy(out=Dall[:, 0, 4 * g:4 * g + 4], in_=Ef[:, 127:512:128]),
                          reads=[t_Ef], writes=[t_D[g]])
                    sc.op("pool", lambda e, g=g: e.tensor_copy(out=Dall[:, 1, 4 * g:4 * g + 4], in_=Eb[:, 0:512:128]),
                          reads=[t_Eb], writes=[t_D[g]])
                    for (w_, t_w, bank) in ((wq, t_wq, 2), (wk, t_wk, 3)):
                        for k in range(KC):
                            sc.op("pe", lambda e, w_=w_, bank=bank, k=k, tok=tok: e.matmul(
                                pb[bank][:, :], lhsT=w_[:, k, :], rhs=hT[:, k, tok], start=(k == 0), stop=(k == KC - 1)),
                                reads=[t_w] + hT_g[g], writes=[pbt[bank]], signal=(k == KC - 1))
                    sc.op("dve", lambda e, tok=tok: e.scalar_tensor_tensor(out=qf[:, tok], in0=pb[2][:, :], scalar=qscale,
                                                                           in1=Ef[:], op0=ALU.mult, op1=ALU.mult),
                          reads=[pbt[2], t_Ef], writes=[t_qk[g]])
                    sc.op("dve", lambda e, tok=tok: e.tensor_tensor(out=kf[:, tok], in0=pb[3][:, :], in1=Eif[:], op=ALU.mult),
                          reads=[pbt[3], t_Eif], writes=[t_qk[g]])
                    sc.op("dve", lambda e, tok=tok: e.scalar_tensor_tensor(out=qb[:, tok], in0=pb[2][:, :], scalar=qscale,
                                                                           in1=Eb[:], op0=ALU.mult, op1=ALU.mult),
                          reads=[pbt[2], t_Eb], writes=[t_qk[g]])
                    sc.op("dve", lambda e, tok=tok: e.tensor_tensor(out=kb[:, tok], in0=pb[3][:, :], in1=Eib[:], op=ALU.mult),
                          reads=[pbt[3], t_Eib], writes=[t_qk[g]])
                def B_pair(tp):
                    bank = 6 + (tp % 2)
                    for j in range(2):
                        t = 2 * tp + j
                        for k in range(KC):
                            sc.op("pe", lambda e, j=j, t=t, k=k: e.matmul(
                                pb[bank][:, j * 256:(j + 1) * 256], lhsT=hT[:, k, t * 128:(t + 1) * 128], rhs=wv[:, k, :],
                                start=(k == 0), stop=(k == KC - 1)),
                                reads=[t_wv, t_hT[t]], writes=[pbt[bank]], signal=(k == KC - 1))
                    sc.op("act", lambda e: e.activation(
                        out=v_h[:, 2 * tp:2 * tp + 2, :], in_=pb[bank][:, :].rearrange("p (j v) -> p j v", j=2), func=AF.Copy),
                        reads=[pbt[bank]], writes=[t_v[tp]])

                def K_scale(n, src, dirn):
                    ch = slice(n * 128, (n + 1) * 128)
                    g = n // 4
                    kout, t_ko = kor.next()
                    sc.op("act", lambda e: e.activation(out=kout[:], in_=src[:, ch], func=AF.Copy,
                                                        scale=Dall[:, dirn, n:n + 1]),
                          reads=[t_qk[g], t_D[g]], writes=[t_ko])
                    return kout, t_ko

                def K_tr(kout, t_ko, pview, pT):
                    ktok, t_kt = ktr.next()
                    sc.op("pe", lambda e: e.transpose(pview, kout[:], ident[:]),
                          reads=[t_ko, t_ident], writes=[pT])
                    sc.op("dve", lambda e: e.tensor_copy(out=ktok[:], in_=pview),
                          reads=[pT], writes=[t_kt])
                    return ktok, t_kt

                def K_stage(n, src, dirn, pview, pT, eng="act"):
                    kout, t_ko = K_scale(n, src, dirn)
                    return K_tr(kout, t_ko, pview, pT)

                s_old, t_so = s32r.next()
                sc.op("pool", lambda e, s_old=s_old: e.memset(s_old[:], 0.0), writes=[t_so])
                sc.op("pool", lambda e: e.memset(sbs[:, NT - 1, :], 0.0), writes=[t_sbs[NT - 1]])
                LC = 3
                kt_q = {}
                for m in range(NT - 1, max(NT - 1 - LC, 0), -1):
                    kt_q[m] = K_stage(m, kb, 1, pbh[m % 2][:, 0:128], pbt[m % 2])
                NP = NT // 2
                B_pair(NP - 1)
                pend_copy = []
                for tp in range(NP - 1, -1, -1):
                    ko_of = {}
                    for n in (2 * tp + 1, 2 * tp):
                        if n >= 1 and n - LC >= 1:
                            ko_of[n - LC] = K_scale(n - LC, kb, 1)
                    while pend_copy:
                        pend_copy.pop()()
                    if tp - 1 >= 0:
                        B_pair(tp - 1)
                    for n in (2 * tp + 1, 2 * tp):
                        if n < 1:
                            continue
                        g = n // 4
                        bs = 2 + (n % 4)
                        ktok, t_kt = kt_q.pop(n)
                        if n - LC >= 1:
                            ko_ = ko_of.pop(n - LC)
                            kt_q[n - LC] = K_tr(ko_[0], ko_[1], pbh[(n - LC) % 2][:, 0:128], pbt[(n - LC) % 2])
                        s_new, t_sn = s32r.next()
                        sc.op("pe", lambda e, bs=bs, ktok=ktok, n=n: e.matmul(pb[bs][:, 0:256], lhsT=ktok[:], rhs=v_h[:, n, :],
                                                                              start=True, stop=True),
                              reads=[t_kt, t_v[n // 2]], writes=[pbt[bs]])
                        sc.op("dve", lambda e, bs=bs, s_old=s_old, s_new=s_new, n=n: e.scalar_tensor_tensor(
                            out=s_new[:], in0=s_old[:], scalar=Dall[:, 1, n:n + 1], in1=pb[bs][:, 0:256],
                            op0=ALU.mult, op1=ALU.add),
                            reads=[t_so, t_D[g], pbt[bs]], writes=[t_sn])
                        if n % 2 == 0:
                            pend_copy.append(lambda s_new=s_new, n=n, t_sn=t_sn: sc.op(
                                "act", lambda e: e.activation(out=sbs[:, n - 1, :], in_=s_new[:], func=AF.Copy),
                                reads=[t_sn], writes=[t_sbs[n - 1]]))
                        else:
                            sc.op("pool", lambda e, s_new=s_new, n=n: e.tensor_copy(out=sbs[:, n - 1, :], in_=s_new[:]),
                                  reads=[t_sn], writes=[t_sbs[n - 1]])
                        s_old, t_so = s_new, t_sn
                while pend_copy:
                    pend_copy.pop()()
                if h + 1 < 4:
                    load_qkv(h + 1)
                s_old, t_so = s32r.next()
                sc.op("pool", lambda e, s_old=s_old: e.memset(s_old[:], 0.0), writes=[t_so])
                sfbf, t_sf = sfr.next()
                sc.op("pool", lambda e, sfbf=sfbf: e.memset(sfbf[:], 0.0), writes=[t_sf])
                att_of = {}
                z_of = {}
                y_of = {}

                def A_stage(n):
                    ch = slice(n * 128, (n + 1) * 128)
                    g = n // 4
                    ba = 0
                    attm, t_at = atr.next()
                    sc.op("pe", lambda e: e.matmul(pb[ba][:, 0:128], lhsT=kf[:, ch], rhs=qf[:, ch], start=True, stop=True),
                          reads=[t_qk[g]], writes=[pbt[ba]], signal=False)
                    sc.op("pe", lambda e: e.matmul(pb[ba][:, 128:256], lhsT=kb[:, ch], rhs=qb[:, ch], start=True, stop=True),
                          reads=[t_qk[g]], writes=[pbt[ba]])
                    sc.op("dve", lambda e: e.tensor_tensor(out=attm[:], in0=pb[ba][:, 0:256], in1=mask[:], op=ALU.mult),
                          reads=[pbt[ba], t_mask], writes=[t_at])
                    att_of[n] = (attm, t_at)

                def ZO_stage(n, sfbf, t_sf):
                    ch = slice(n * 128, (n + 1) * 128)
                    g = n // 4
                    bz_ = 3 + (n % 2)
                    bo = 1 + (n % 2)
                    for k in range(KC):
                        sc.op("pe", lambda e, k=k: e.matmul(pb[bz_][:, 0:256], lhsT=hT[:, k, ch], rhs=wz[:, k, :],
                                                            start=(k == 0), stop=(k == KC - 1)),
                              reads=[t_wz, t_hT[n]], writes=[pbt[bz_]], signal=(k == KC - 1))
                    attm, t_at = att_of.pop(n)
                    sc.op("pe", lambda e: e.matmul(pb[bo][:, 0:256], lhsT=attm[:, 0:128], rhs=v_h[:, n, :], start=True, stop=False),
                          reads=[t_at, t_v[n // 2]], writes=[pbt[bo]], signal=False)
                    sc.op("pe", lambda e: e.matmul(pb[bo][:, 0:256], lhsT=attm[:, 128:256], rhs=v_h[:, n, :], start=False, stop=False),
                          reads=[t_at, t_v[n // 2]], writes=[pbt[bo]], signal=False)
                    sc.op("pe", lambda e: e.matmul(pb[bo][:, 0:256], lhsT=qb[:, ch], rhs=sbs[:, n, :], start=False, stop=False),
                          reads=[t_qk[g], t_sbs[n]], writes=[pbt[bo]], signal=False)
                    sc.op("pe", lambda e: e.matmul(pb[bo][:, 0:256], lhsT=qf[:, ch], rhs=sfbf[:], start=False, stop=True),
                          reads=[t_qk[g], t_sf], writes=[pbt[bo]])
                    sg, t_sg = sgr3.next()
                    t2, t_t2 = t2r3.next()
                    t1, t_t1 = t1r3.next()
                    ytok, t_yt = ytr.next()
                    ss_, ln_, rstd, ts = new_stat()
                    sc.op("act", lambda e: e.activation(out=sg[:], in_=pb[bz_][:, 0:256], func=AF.Exp, scale=-1.0),
                          reads=[pbt[bz_]], writes=[t_sg])
                    sc.op("act", lambda e: e.activation(out=junk[:, 0:256], in_=pb[bo][:, 0:256], func=AF.Square, accum_out=ss_),
                          reads=[pbt[bo]], writes=[t_junk, ts])
                    sc.op("act", lambda e: e.activation(out=sg[:], in_=sg[:], func=AF.Ln, bias=1.0, scale=1.0),
                          reads=[t_sg], writes=[t_sg])
                    sc.op("act", lambda e: e.activation(out=ln_, in_=ss_, func=AF.Ln, bias=EPS, scale=1.0 / 256),
                          reads=[ts], writes=[ts])
                    sc.op("act", lambda e: e.activation(out=sg[:], in_=sg[:], func=AF.Exp, scale=-1.0),
                          reads=[t_sg], writes=[t_sg])
                    sc.op("act", lambda e: e.activation(out=rstd, in_=ln_, func=AF.Exp, scale=-0.5),
                          reads=[ts], writes=[ts])
                    sc.op("dve", lambda e: e.tensor_tensor(out=t2[:], in0=pb[bz_][:, 0:256], in1=sg[:], op=ALU.mult),
                          reads=[pbt[bz_], t_sg], writes=[t_t2])
                    sc.op("dve", lambda e: e.scalar_tensor_tensor(out=t1[:], in0=pb[bo][:, 0:256], scalar=rstd, in1=gng[:],
                                                                  op0=ALU.mult, op1=ALU.mult),
                          reads=[pbt[bo], ts, t_gng], writes=[t_t1])
                    sc.op("pool", lambda e: e.tensor_tensor(out=ytok[:], in0=t1[:], in1=t2[:], op=ALU.mult),
                          reads=[t_t1, t_t2], writes=[t_yt])
                    y_of[n] = (ytok, t_yt)

                def S_stage(n, ktok, t_kt, s_old, t_so):
                    g = n // 4
                    s_new, t_sn = s32r.next()
                    sfn, t_sfn = sfr.next()
                    sc.op("pe", lambda e: e.matmul(pb[5][:, 0:256], lhsT=ktok[:], rhs=v_h[:, n, :], start=True, stop=True),
                          reads=[t_kt, t_v[n // 2]], writes=[pbt[5]])
                    sc.op("dve", lambda e: e.scalar_tensor_tensor(out=s_new[:], in0=s_old[:], scalar=Dall[:, 0, n:n + 1],
                                                                  in1=pb[5][:, 0:256], op0=ALU.mult, op1=ALU.add),
                          reads=[t_so, t_D[g], pbt[5]], writes=[t_sn])
                    sc.op("pool", lambda e: e.tensor_copy(out=sfn[:], in_=s_new[:]),
                          reads=[t_sn], writes=[t_sfn])
                    return s_new, t_sn, sfn, t_sfn

                ystate = {"stg": None}

                def Y_stage(n):
                    g = n // 4
                    ytok, t_yt = y_of.pop(n)
                    for j in range(2):
                        sc.op("pe", lambda e, j=j: e.transpose(pbh[7][:, 256 + j * 128:256 + (j + 1) * 128],
                                                               ytok[:, j * 128:(j + 1) * 128], ident[:]),
                              reads=[t_yt, t_ident], writes=[pbt[7]], signal=(j == 1))
                    if n % 4 == 0:
                        ystate["stg"] = ysr.next()
                    ystg, t_ys = ystate["stg"]
                    sc.op("dve", lambda e: e.tensor_copy(
                        out=ystg[:, :, (n % 4) * 128:(n % 4 + 1) * 128],
                        in_=pbh[7][:, 256:512].rearrange("p (j t) -> p j t", j=2)),
                        reads=[pbt[7]], writes=[t_ys])
                    if n % 4 == 3:
                        sc.dma("sp", yT_v[:, 2 * h:2 * h + 2, g * 512:(g + 1) * 512], ystg[:], reads=[t_ys], tsem=t_ys)

                LK = 3
                kq = {}
                sf_of = {0: (sfbf, t_sf)}
                for m in range(0, min(LK, NT - 1)):
                    kq[m] = K_stage(m, kf, 0, pbh[6][:, 0:128], pbt[6], eng="act")
                if NT > 1:
                    kt_cur = kq.pop(0)
                    s_old, t_so, sfn_, t_sfn_ = S_stage(0, kt_cur[0], kt_cur[1], s_old, t_so)
                    sf_of[1] = (sfn_, t_sfn_)
                A_stage(0)
                for n in range(NT):
                    ko_ = K_scale(n + LK, kf, 0) if n + LK <= NT - 2 else None
                    if n + 1 <= NT - 2:
                        kt_cur = kq.pop(n + 1)
                        s_old, t_so, sfn_, t_sfn_ = S_stage(n + 1, kt_cur[0], kt_cur[1], s_old, t_so)
                        sf_of[n + 2] = (sfn_, t_sfn_)
                    if n + 1 < NT:
                        A_stage(n + 1)
                    sfb_, t_sfb_ = sf_of.pop(n)
                    ZO_stage(n, sfb_, t_sfb_)
                    if n >= 2:
                        Y_stage(n - 2)
                    if ko_ is not None:
                        kq[n + LK] = K_tr(ko_[0], ko_[1], pbh[6][:, 0:128], pbt[6])
                if NT >= 2:
                    Y_stage(NT - 2)
                Y_stage(NT - 1)
                if h + 1 < 4:
                    load_z(h + 1)
            sc.wait_all("pool", [b[1] for b in ysr.bufs])
            sc.op("pool", lambda e: e.memset(dummy[:], 0.0), writes=[t_dummy])
            sc.barrier()

        t_yT = T("yT_dram")
        with ExitStack() as p4:
            wo = sb("wo", [128, 16, D], BF16, p4); t_wo = [T("wo%d" % i) for i in range(4)]
            yr = Ring([(sb("yt%d" % i, [128, 16, 512], BF16, p4), [T("yt%d_%d" % (i, j)) for j in range(4)]) for i in range(2)])
            xr4 = Ring([(sb("x4_%d" % i, [128, D], F32, p4), T("x4_%d" % i)) for i in range(3)])
            rr = Ring([(sb("r4_%d" % i, [128, D], F32, p4), T("r4_%d" % i)) for i in range(3)])
            outs = []
            yt, t_yt4 = None, None
            for t in range(NT):
                g = t // 4
                if t % 4 == 0:
                    yt, t_yt4 = yr.next()
                    for i in range(4):
                        if t == 0:
                            sc.dma("sp", wo[:, 4 * i:4 * i + 4, :], wob_v[:, 4 * i:4 * i + 4, :], reads=[t_wob[i]], writes=[t_wo[i]])
                        sc.dma("sp", yt[:, 4 * i:4 * i + 4, :], yT_v[:, 4 * i:4 * i + 4, g * 512:(g + 1) * 512], writes=[t_yt4[i]])
                xt, t_xt = xr4.next()
                r, t_r = rr.next()
                sc.dma("sp", xt[:], x_d[t * 128:(t + 1) * 128, :], writes=[t_xt])
                tt = t % 4
                for half in range(2):
                    bank = 2 * (t % 2) + half
                    for c in range(16):
                        sc.op("pe", lambda e, bank=bank, c=c, tt=tt, half=half, yt=yt: e.matmul(
                            pb[bank][:, :], lhsT=yt[:, c, tt * 128:(tt + 1) * 128], rhs=wo[:, c, half * 512:(half + 1) * 512],
                            start=(c == 0), stop=(c == 15)),
                            reads=[t_yt4[c // 4], t_wo[c // 4]], writes=[pbt[bank]], signal=(c == 15))
                    sc.op("dve", lambda e, bank=bank, half=half, r=r, xt=xt: e.tensor_tensor(
                        out=r[:, half * 512:(half + 1) * 512], in0=pb[bank][:, :], in1=xt[:, half * 512:(half + 1) * 512], op=ALU.add),
                        reads=[pbt[bank], t_xt], writes=[t_r])
                rstd, ts = rms_rstd(r[:], [t_r], D, junk[:])
                sc.op("dve", lambda e, r=r, rstd=rstd: e.scalar_tensor_tensor(out=r[:], in0=r[:], scalar=rstd, in1=fg[:],
                                                                              op0=ALU.mult, op1=ALU.mult),
                      reads=[t_r, ts, t_fg], writes=[t_r])
                sc.dma("pool", out_d[t * 128:(t + 1) * 128, :], r[:], reads=[t_r], tsem=t_r)
            sc.wait_all("pool", [b[1] for b in rr.bufs])
        sc.emit()
    return nc


_NC_CACHE = {}


def kernel(x, norm_g, w_in, w_gk_f, b_gk_f, w_gk_b, b_gk_b, gla_norm_g, conv_w, conv_b, w_out, final_g):
    x = np.asarray(x, dtype=np.float32)
    B, S, _ = x.shape
    if S not in _NC_CACHE:
        _NC_CACHE[S] = build(S)
    nc = _NC_CACHE[S]
    f = lambda a: np.ascontiguousarray(np.asarray(a, dtype=np.float32))
    shared = {
        "norm_g": f(norm_g)[0], "w_in": f(w_in)[0], "w_gk_f": f(w_gk_f)[0], "b_gk_f": f(b_gk_f)[0],
        "w_gk_b": f(w_gk_b)[0], "b_gk_b": f(b_gk_b)[0], "gla_norm_g": f(gla_norm_g)[0],
        "conv_w": f(conv_w)[0], "conv_b": f(conv_b)[0], "w_out": f(w_out)[0], "final_g": f(final_g),
    }
    in_maps = []
    for b in range(B):
        m = dict(shared)
        m["x"] = np.ascontiguousarray(x[b])
        in_maps.append(m)
    res = run_bass_kernel_spmd(nc, in_maps, core_ids=list(range(B)))
    return np.stack([np.asarray(r["out"], dtype=np.float32) for r in res.results], axis=0)
```

```python
import numpy as np
from contextlib import ExitStack
import concourse.bass as bass
import concourse.mybir as mybir
from concourse.bass_utils import run_bass_kernel_spmd

F32 = mybir.dt.float32
BF16 = mybir.dt.bfloat16
AF = mybir.ActivationFunctionType
ALU = mybir.AluOpType

D = 1024
KC = 8
IN_W = 7200
EPS = 1e-6
OQ, OK_, OV, OZA, OLR, OB, OC, OH, OZC = 0, 512, 1024, 2048, 3072, 3104, 4128, 5152, 6176


class T:
    __slots__ = ("name", "w", "r", "dsem", "dcount")

    def __init__(self, name):
        self.name = name
        self.w = None
        self.r = {}
        self.dsem = None
        self.dcount = None


class EngQ:
    def __init__(self, name):
        self.name = name
        self.ops = []
        self.sem = None
        self.count = 0
        self.waited = {}


class Sched:
    ENGS = ("pe", "act", "dve", "pool", "sp")

    def __init__(self, nc, stack):
        self.nc = nc
        self.stack = stack
        self.q = {n: EngQ(n) for n in self.ENGS}
        for n in self.ENGS:
            self.q[n].sem = stack.enter_context(nc.semaphore("sem_" + n))
        self.nsem = 0

    def _need(self, q, ev, waits):
        if ev is None:
            return
        sem, val = ev
        k = id(sem)
        if sem is q.sem and val > q.count:
            return
        if q.waited.get(k, 0) >= val:
            return
        q.waited[k] = val
        waits[k] = (sem, max(val, waits.get(k, (None, 0))[1]))

    def _deps(self, q, reads, writes):
        waits = {}
        for t in reads:
            self._need(q, t.w, waits)
        for t in writes:
            self._need(q, t.w, waits)
            for ev in t.r.values():
                self._need(q, ev, waits)
        return list(waits.values())

    def _mark(self, ev, reads, writes):
        k = id(ev[0])
        for t in reads:
            old = t.r.get(k)
            if old is None or old[1] < ev[1]:
                t.r[k] = ev
        for t in writes:
            t.w = ev
            t.r = {}

    def op(self, eng, fn, reads=(), writes=(), signal=True):
        q = self.q[eng]
        waits = self._deps(q, reads, writes)
        ev = (q.sem, q.count + 1)
        if signal:
            q.count += 1
        sem = q.sem

        def emit(e, fn=fn, waits=waits, signal=signal, sem=sem):
            for (s, v) in waits:
                e.wait_ge(s, v)
            ins = fn(e)
            if signal:
                ins.then_inc(sem, 1)
        q.ops.append(emit)
        self._mark(ev, reads, writes)

    def dma(self, eng, out, in_, reads=(), writes=(), tsem=None, **kw):
        q = self.q[eng]
        waits = self._deps(q, reads, writes)
        t = tsem if tsem is not None else (writes[0] if writes else reads[0])
        if t.dsem is None:
            t.dsem = {}
            t.dcount = {}
        if eng not in t.dsem:
            t.dsem[eng] = self.stack.enter_context(
                self.nc.semaphore("dsem_%d" % self.nsem))
            t.dcount[eng] = 0
            self.nsem += 1
        t.dcount[eng] += 16
        dsem = t.dsem[eng]
        ev = (dsem, t.dcount[eng])

        def emit(e, waits=waits, out=out, in_=in_, sem=dsem, kw=kw):
            for (s, v) in waits:
                e.wait_ge(s, v)
            e.dma_start(out=out, in_=in_, **kw).then_inc(sem, 16)
        q.ops.append(emit)
        self._mark(ev, reads, writes)
        return ev

    def wait_all(self, eng, tiles):
        q = self.q[eng]
        waits = self._deps(q, (), tiles)

        def emit(e, waits=waits):
            for (s, v) in waits:
                e.wait_ge(s, v)
        q.ops.append(emit)

    def barrier(self):
        evs = [(self.q[n].sem, self.q[n].count) for n in self.ENGS if self.q[n].count > 0]
        for n in self.ENGS:
            q = self.q[n]
            waits = {}
            for ev in evs:
                self._need(q, ev, waits)
            wl = list(waits.values())

            def emit(e, wl=wl):
                for (s, v) in wl:
                    e.wait_ge(s, v)
            q.ops.append(emit)

    def emit(self):
        with self.nc.Block() as block:
            @block.tensor
            def _(e):
                for f in self.q["pe"].ops:
                    f(e)

            @block.scalar
            def _(e):
                for f in self.q["act"].ops:
                    f(e)

            @block.vector
            def _(e):
                for f in self.q["dve"].ops:
                    f(e)

            @block.gpsimd
            def _(e):
                for f in self.q["pool"].ops:
                    f(e)

            @block.sync
            def _(e):
                for f in self.q["sp"].ops:
                    f(e)


class Ring:
    def __init__(self, bufs):
        self.bufs = bufs
        self.i = 0

    def next(self):
        b = self.bufs[self.i % len(self.bufs)]
        self.i += 1
        return b


def build(S=4096):
    NT = S // 128
    NG = S // 512
    nc = bass.Bass("TRN2", target_bir_lowering=False, dynamic_dma_scratch_size=12288)
    x_d = nc.dram_tensor("x", [S, D], F32, kind="ExternalInput").ap()
    norm_g_d = nc.dram_tensor("norm_g", [D], F32, kind="ExternalInput").ap()
    w_in_d = nc.dram_tensor("w_in", [D, IN_W], F32, kind="ExternalInput").ap()
    w_gk_f_d = nc.dram_tensor("w_gk_f", [16, 512], F32, kind="ExternalInput").ap()
    b_gk_f_d = nc.dram_tensor("b_gk_f", [512], F32, kind="ExternalInput").ap()
    w_gk_b_d = nc.dram_tensor("w_gk_b", [16, 512], F32, kind="ExternalInput").ap()
    b_gk_b_d = nc.dram_tensor("b_gk_b", [512], F32, kind="ExternalInput").ap()
    gng_d = nc.dram_tensor("gla_norm_g", [256], F32, kind="ExternalInput").ap()
    conv_w_d = nc.dram_tensor("conv_w", [3, D], F32, kind="ExternalInput").ap()
    conv_b_d = nc.dram_tensor("conv_b", [D], F32, kind="ExternalInput").ap()
    w_out_d = nc.dram_tensor("w_out", [2048, D], F32, kind="ExternalInput").ap()
    final_g_d = nc.dram_tensor("final_g", [D], F32, kind="ExternalInput").ap()
    out_d = nc.dram_tensor("out", [S, D], F32, kind="ExternalOutput").ap()
    yT_d = nc.dram_tensor("yT_scratch", [16, 128, S], BF16, kind="Internal").ap()
    wob_d = nc.dram_tensor("wout_bf16", [2048, D], BF16, kind="Internal").ap()

    w_in_v = w_in_d.rearrange("(k p) c -> p k c", p=128)
    w_out_v = w_out_d.rearrange("(c p) d -> p c d", p=128)
    yT_v = yT_d.rearrange("c p s -> p c s")
    wob_v = wob_d.rearrange("(c p) d -> p c d", p=128)

    with ExitStack() as st:
        sc = Sched(nc, st)

        def sb(name, shape, dt, stack=st):
            return stack.enter_context(nc.sbuf_tensor(name, shape, dt))

        pb = []
        pbt = []
        for i in range(8):
            pb.append(st.enter_context(nc.psum_tensor("pb%d" % i, [128, 512], F32)))
            pbt.append(T("pb%d" % i))
        pbh = [p.bitcast(BF16) for p in pb]

        hT = sb("hT", [128, KC, S], BF16)
        t_hT = [T("hT%d" % t) for t in range(NT)]
        ident = sb("ident", [128, 128], BF16); t_ident = T("ident")
        mask = sb("mask", [128, 256], F32); t_mask = T("mask")
        rm_f = sb("rm_f", [128, 512], F32); t_rm = T("rm")
        rm_b = sb("rm_b", [128, 512], F32)
        gcol = sb("gcol", [128, KC], F32); t_gcol = T("gcol")
        gexp = sb("gexp", [128, KC, 128], F32); t_gexp = T("gexp")
        gng = sb("gng", [128, 256], F32); t_gng = T("gng")
        fg = sb("fg", [128, D], F32); t_fg = T("fg")
        nbias = sb("nbias", [128, 8], F32); t_nbias = T("nbias")
        cw = sb("cw", [128, 3, KC], F32); t_cw = T("cw")
        cb = sb("cb", [128, KC], F32); t_cb = T("cb")
        wg = sb("wg", [32, 2, 512], BF16); t_wg = T("wg")
        lrT = sb("lrT", [32, S], BF16); t_lrT = [T("lrT%d" % g) for g in range(NG)]
        wlr = sb("wlr", [128, KC, 32], BF16); t_wlr = T("wlr")
        junk = sb("junk", [128, D], BF16); t_junk = T("junk")
        dummy = sb("dummy_t", [128, 8], F32); t_dummy = T("dummy")

        sc.op("pool", lambda e: e.memset(ident[:], 0.0), writes=[t_ident])
        sc.op("pool", lambda e: e.affine_select(out=ident[:], in_=ident[:], compare_op=ALU.not_equal,
                                                fill=1.0, base=0, pattern=[[-1, 128]], channel_multiplier=1),
              reads=[t_ident], writes=[t_ident])
        sc.op("pool", lambda e: e.memset(mask[:], 1.0), writes=[t_mask])
        sc.op("pool", lambda e: e.affine_select(out=mask[:, 0:128], in_=mask[:, 0:128], compare_op=ALU.is_ge,
                                                fill=0.0, base=0, pattern=[[1, 128]], channel_multiplier=-1),
              reads=[t_mask], writes=[t_mask])
        sc.op("pool", lambda e: e.affine_select(out=mask[:, 128:256], in_=mask[:, 128:256], compare_op=ALU.is_gt,
                                                fill=0.0, base=0, pattern=[[-1, 128]], channel_multiplier=1),
              reads=[t_mask], writes=[t_mask])
        sc.op("pool", lambda e: e.memset(rm_f[:], 1.0), writes=[t_rm])
        sc.op("pool", lambda e: e.memset(rm_b[:], 1.0), writes=[t_rm])
        sc.op("pool", lambda e: e.memset(rm_f[:, 0:512:128], 0.0), writes=[t_rm])
        sc.op("pool", lambda e: e.memset(rm_b[:, 127:512:128], 0.0), writes=[t_rm])
        sc.op("pool", lambda e: e.memset(wg[:], 0.0), writes=[t_wg])
        sc.op("pool", lambda e: e.memset(gexp[:], 1.0), writes=[t_gexp])

        sc.dma("sp", gcol[:], norm_g_d.rearrange("(c p) -> p c", p=128), writes=[t_gcol],
               allow_slow_non_contiguous=True)
        for c in range(KC):
            sc.op("dve", lambda e, c=c: e.tensor_scalar(out=gexp[:, c, :], in0=gexp[:, c, :],
                                                        scalar1=gcol[:, c:c + 1], scalar2=None, op0=ALU.mult),
                  reads=[t_gcol, t_gexp], writes=[t_gexp])

        def load_small_consts():
            sc.dma("pool", wg[0:16, 0, :], w_gk_f_d, writes=[t_wg])
            sc.dma("pool", wg[16:32, 1, :], w_gk_b_d, writes=[t_wg])
            sc.dma("pool", wlr[:], w_in_v[:, :, OLR:OLR + 32], writes=[t_wlr])
            sc.dma("pool", gng[:], gng_d.partition_broadcast(128), writes=[t_gng])
            sc.dma("pool", fg[:], final_g_d.partition_broadcast(128), writes=[t_fg])
            sc.dma("pool", nbias[:, 0:4], b_gk_f_d.rearrange("(h p) -> p h", p=128), writes=[t_nbias],
                   allow_slow_non_contiguous=True)
            sc.dma("pool", nbias[:, 4:8], b_gk_b_d.rearrange("(h p) -> p h", p=128), writes=[t_nbias],
                   allow_slow_non_contiguous=True)
            sc.dma("pool", cw[:], conv_w_d.rearrange("k (c p) -> p k c", p=128), writes=[t_cw],
                   allow_slow_non_contiguous=True)
            sc.dma("pool", cb[:], conv_b_d.rearrange("(c p) -> p c", p=128), writes=[t_cb],
                   allow_slow_non_contiguous=True)

        stat = sb("stat", [128, 64], F32)
        t_stat = [T("stat%d" % i) for i in range(16)]
        stat_i = [0]

        def new_stat():
            i = stat_i[0] % 16
            stat_i[0] += 1
            return (stat[:, 4 * i:4 * i + 1], stat[:, 4 * i + 1:4 * i + 2], stat[:, 4 * i + 2:4 * i + 3], t_stat[i])

        def rms_rstd(src_ap, src_ts, n_elems, junk_ap):
            ss, lnv, rstd, ts = new_stat()
            sc.op("act", lambda e: e.activation(out=junk_ap, in_=src_ap, func=AF.Square, accum_out=ss),
                  reads=src_ts, writes=[t_junk, ts])
            sc.op("act", lambda e: e.activation(out=lnv, in_=ss, func=AF.Ln, bias=EPS, scale=1.0 / n_elems),
                  reads=[ts], writes=[ts])
            sc.op("act", lambda e: e.activation(out=rstd, in_=lnv, func=AF.Exp, scale=-0.5),
                  reads=[ts], writes=[ts])
            return rstd, ts

        wq = sb("wq", [128, KC, 128], BF16); t_wq = T("wq")
        wk = sb("wk", [128, KC, 128], BF16); t_wk = T("wk")
        wv = sb("wv", [128, KC, 256], BF16); t_wv = T("wv")
        wz = sb("wz", [128, KC, 256], BF16); t_wz = T("wz")
        p12 = ExitStack()
        p12.__enter__()
        p1 = p12
        xr = Ring([(sb("xt%d" % i, [128, D], F32, p1), T("xt%d" % i)) for i in range(6)])
        xnr = Ring([(sb("xn%d" % i, [128, D], BF16, p1), T("xn%d" % i)) for i in range(3)])

        def p1_tile(t):
            xt, t_xt = xr.next()
            xn, t_xn = xnr.next()
            bank = t % 4
            sc.dma("sp", xt[:], x_d[t * 128:(t + 1) * 128, :], writes=[t_xt])
            rstd, ts = rms_rstd(xt[:], [t_xt], D, junk[:])
            XS = 384
            sc.op("act", lambda e: e.activation(out=xn[:, 0:XS], in_=xt[:, 0:XS], func=AF.Copy, scale=rstd),
                  reads=[t_xt, ts], writes=[t_xn])
            sc.op("dve", lambda e: e.tensor_scalar(out=xn[:, XS:D], in0=xt[:, XS:D], scalar1=rstd, scalar2=None, op0=ALU.mult),
                  reads=[t_xt, ts], writes=[t_xn])
            for c in range(KC):
                sc.op("pe", lambda e, c=c: e.transpose(
                    pbh[bank][:, c * 128:(c + 1) * 128], xn[:, c * 128:(c + 1) * 128], ident[:]),
                    reads=[t_xn, t_ident], writes=[pbt[bank]], signal=(c == KC - 1))
            sc.op("dve", lambda e: e.tensor_tensor(
                out=hT[:, :, t * 128:(t + 1) * 128],
                in0=pbh[bank][:, :].rearrange("p (c t) -> p c t", c=KC),
                in1=gexp[:], op=ALU.mult),
                reads=[pbt[bank], t_gexp], writes=[t_hT[t]])


        def load_qkv(h):
            sc.dma("pool", wq[:], w_in_v[:, :, OQ + h * 128:OQ + (h + 1) * 128], writes=[t_wq])
            sc.dma("pool", wk[:], w_in_v[:, :, OK_ + h * 128:OK_ + (h + 1) * 128], writes=[t_wk])
            sc.dma("pool", wv[:], w_in_v[:, :, OV + h * 256:OV + (h + 1) * 256], writes=[t_wv])

        def load_z(h):
            sc.dma("pool", wz[:], w_in_v[:, :, OZA + h * 256:OZA + (h + 1) * 256], writes=[t_wz])

        hT_g = [t_hT[4 * g:4 * g + 4] for g in range(NG)]

        if True:
            p2 = p12
            wcr = Ring([(sb("wc%d" % i, [128, KC, 4, 128], BF16, p2), [T("wc%d_%d" % (i, j)) for j in range(4)])
                        for i in range(2)])
            u_full = sb("u_full", [128, S + 2], F32, p2)
            t_u = [T("u%d" % g) for g in range(NG)]
            t_upad = T("upad")
            hcr = Ring([(sb("hcs%d" % i, [128, 512], F32, p2), T("hcs%d" % i)) for i in range(2)])
            sgr = Ring([(sb("sg%d" % i, [128, 512], F32, p2), T("sg%d" % i)) for i in range(2)])
            bzr = Ring([(sb("bz%d" % i, [128, 512], F32, p2), T("bz%d" % i)) for i in range(3)])
            t1r = Ring([(sb("t1_%d" % i, [128, 512], F32, p2), T("t1_%d" % i)) for i in range(2)])
            ycr = Ring([(sb("yc%d" % i, [128, S], BF16, p2), T("yc%d" % i)) for i in range(2)])
            sc.op("pool", lambda e: e.memset(u_full[:, 0:1], 0.0), writes=[t_upad])
            sc.op("pool", lambda e: e.memset(u_full[:, S + 1:S + 2], 0.0), writes=[t_upad])
            offs = [OB, OC, OH, OZC]
            t_wob = [T("wob%d" % i) for i in range(4)]

            def load_wc(c):
                wc, t_wc = wcr.next()
                for part in range(4):
                    sc.dma("pool", wc[:, :, part, :], w_in_v[:, :, offs[part] + c * 128: offs[part] + (c + 1) * 128],
                           writes=[t_wc[part]])
                return wc, t_wc

            wc_next = load_wc(0)
            load_small_consts()
            for c in range(KC):
                wc, t_wc = wc_next
                if c + 1 < KC:
                    wc_next = load_wc(c + 1)
                yc, t_yc = ycr.next()
                bz_of = {}

                def conv_group(g, c=c, yc=yc, t_yc=t_yc):
                    c0 = 1 + g * 512
                    t1, t_t1 = t1r.next()
                    bz, t_bz = bz_of.pop(g)
                    rd = [t_u[g], t_upad]
                    if g > 0:
                        rd.append(t_u[g - 1])
                    if g < NG - 1:
                        rd.append(t_u[g + 1])
                    sc.op("pool", lambda e: e.tensor_scalar(out=t1[:], in0=u_full[:, c0:c0 + 512],
                                                            scalar1=cw[:, 1, c:c + 1], scalar2=cb[:, c:c + 1],
                                                            op0=ALU.mult, op1=ALU.add),
                          reads=rd + [t_cw, t_cb], writes=[t_t1])
                    sc.op("dve", lambda e: e.scalar_tensor_tensor(out=t1[:], in0=u_full[:, c0 - 1:c0 + 511],
                                                                  scalar=cw[:, 0, c:c + 1], in1=t1[:],
                                                                  op0=ALU.mult, op1=ALU.add),
                          reads=rd + [t_cw, t_t1], writes=[t_t1])
                    sc.op("dve", lambda e: e.scalar_tensor_tensor(out=t1[:], in0=u_full[:, c0 + 1:c0 + 513],
                                                                  scalar=cw[:, 2, c:c + 1], in1=t1[:],
                                                                  op0=ALU.mult, op1=ALU.add),
                          reads=rd + [t_cw, t_t1], writes=[t_t1])
                    sc.op("pool", lambda e: e.tensor_tensor(out=yc[:, g * 512:(g + 1) * 512], in0=t1[:], in1=bz[:],
                                                            op=ALU.mult),
                          reads=[t_t1, t_bz], writes=[t_yc])

                if c == min(1, KC - 1):
                    sc.op("pool", lambda e: e.tensor_scalar(out=nbias[:], in0=nbias[:], scalar1=-1.0, scalar2=None,
                                                            op0=ALU.mult), reads=[t_nbias], writes=[t_nbias])
                if 1 <= c <= 4:
                    i = c - 1
                    sc.dma("pool", wob_d[512 * i:512 * (i + 1), :], w_out_d[512 * i:512 * (i + 1), :], writes=[t_wob[i]])
                if c == KC - 2:
                    load_qkv(0)
                if c == KC - 1:
                    load_z(0)
                    for g_ in range(NG):
                        for k in range(KC):
                            sc.op("pe", lambda e, k=k, g_=g_: e.matmul(pb[0][0:32, :], lhsT=wlr[:, k, :],
                                                                       rhs=hT[:, k, g_ * 512:(g_ + 1) * 512],
                                                                       start=(k == 0), stop=(k == KC - 1)),
                                  reads=[t_wlr] + hT_g[g_], writes=[pbt[0]], signal=(k == KC - 1))
                        sc.op("act", lambda e, g_=g_: e.activation(out=lrT[:, g_ * 512:(g_ + 1) * 512], in_=pb[0][0:32, :], func=AF.Copy),
                              reads=[pbt[0]], writes=[t_lrT[g_]])
                for g in range(NG):
                    if c == 0 and g == 0:
                        for t in range(0, 4):
                            p1_tile(t)
                    b0 = 4 if c == 0 else 4 * (g % 2)
                    for part in range(4):
                        for k in range(KC):
                            sc.op("pe", lambda e, part=part, k=k, b0=b0, g=g, wc=wc: e.matmul(
                                pb[b0 + part][:, :], lhsT=wc[:, k, part, :], rhs=hT[:, k, g * 512:(g + 1) * 512],
                                start=(k == 0), stop=(k == KC - 1)),
                                reads=[t_wc[part]] + hT_g[g], writes=[pbt[b0 + part]], signal=(k == KC - 1))
                        if c == 0 and g + 1 < NG:
                            p1_tile(4 * (g + 1) + part)
                    p_b, p_c, p_h, p_z = (pb[b0 + i] for i in range(4))
                    tb, tcg, th, tz = (pbt[b0 + i] for i in range(4))
                    sg, t_sg = sgr.next()
                    hcs, t_hcs = hcr.next()
                    bz, t_bz = bzr.next()
                    bz_of[g] = (bz, t_bz)
                    sc.op("act", lambda e, sg=sg, p_z=p_z: e.activation(out=sg[:], in_=p_z[:, :], func=AF.Exp, scale=-1.0),
                          reads=[tz], writes=[t_sg])
                    sc.op("act", lambda e, sg=sg: e.activation(out=sg[:], in_=sg[:], func=AF.Ln, bias=1.0, scale=1.0),
                          reads=[t_sg], writes=[t_sg])
                    sc.op("act", lambda e, sg=sg: e.activation(out=sg[:], in_=sg[:], func=AF.Exp, scale=-1.0),
                          reads=[t_sg], writes=[t_sg])
                    sc.op("act", lambda e, hcs=hcs, p_h=p_h: e.activation(out=hcs[:], in_=p_h[:, :], func=AF.Copy),
                          reads=[th], writes=[t_hcs])
                    sc.op("dve", lambda e, bz=bz, p_b=p_b, sg=sg: e.tensor_tensor(out=bz[:], in0=p_b[:, :], in1=sg[:], op=ALU.mult),
                          reads=[tb, t_sg], writes=[t_bz])
                    sc.op("dve", lambda e, bz=bz, p_z=p_z: e.tensor_tensor(out=bz[:], in0=p_z[:, :], in1=bz[:], op=ALU.mult),
                          reads=[tz, t_bz], writes=[t_bz])
                    sc.op("dve", lambda e, g=g, p_c=p_c, hcs=hcs: e.tensor_tensor(
                        out=u_full[:, 1 + g * 512:1 + (g + 1) * 512], in0=p_c[:, :], in1=hcs[:], op=ALU.mult),
                        reads=[tcg, t_hcs], writes=[t_u[g]])
                    if g > 0:
                        conv_group(g - 1)
                conv_group(NG - 1)
                sc.dma("sp", yT_v[:, 8 + c, :], yc[:], reads=[t_yc], tsem=t_yc)
            sc.wait_all("pool", [b[1] for b in ycr.bufs])
            sc.op("pool", lambda e: e.memset(dummy[:], 0.0), writes=[t_dummy])
            sc.barrier()
        p12.__exit__(None, None, None)

        with ExitStack() as p3:
            qf = sb("qf", [128, S], BF16, p3); qb = sb("qb", [128, S], BF16, p3)
            kf = sb("kf", [128, S], BF16, p3); kb = sb("kb", [128, S], BF16, p3)
            t_qk = [T("qk%d" % g) for g in range(NG)]
            v_h = sb("v_h", [128, NT, 256], BF16, p3)
            t_v = [T("v%d" % (i)) for i in range(NT // 2)]
            sbs = sb("sbs", [128, NT, 256], BF16, p3)
            t_sbs = [T("sbs%d" % i) for i in range(NT)]
            spf = sb("spf", [128, 512], F32, p3); t_spf = T("spf")
            spb = sb("spb", [128, 512], F32, p3); t_spb = T("spb")
            cumf = sb("cumf", [128, 512], F32, p3); t_cumf = T("cumf")
            cumb = sb("cumb", [128, 512], F32, p3); t_cumb = T("cumb")
            Ef = sb("Ef", [128, 512], F32, p3); t_Ef = T("Ef")
            Eb = sb("Eb", [128, 512], F32, p3); t_Eb = T("Eb")
            Eif = sb("Eif", [128, 512], F32, p3); t_Eif = T("Eif")
            Eib = sb("Eib", [128, 512], F32, p3); t_Eib = T("Eib")
            Dall = sb("Dall", [128, 2, NT], F32, p3); t_D = [T("D%d" % g) for g in range(NG)]
            s32r = Ring([(sb("S32_%d" % i, [128, 256], F32, p3), T("S32_%d" % i)) for i in range(3)])
            sfr = Ring([(sb("sfbf%d" % i, [128, 256], BF16, p3), T("sfbf%d" % i)) for i in range(4)])
            ktr = Ring([(sb("ktok%d" % i, [128, 128], BF16, p3), T("ktok%d" % i)) for i in range(6)])
            kor = Ring([(sb("kout%d" % i, [128, 128], BF16, p3), T("kout%d" % i)) for i in range(5)])
            atr = Ring([(sb("attm%d" % i, [128, 256], BF16, p3), T("attm%d" % i)) for i in range(3)])
            pbt7a = T("pb7a"); pbt7b = T("pb7b")
            sgr3 = Ring([(sb("sgz%d" % i, [128, 256], F32, p3), T("sgz%d" % i)) for i in range(2)])
            t1r3 = Ring([(sb("t1z%d" % i, [128, 256], F32, p3), T("t1z%d" % i)) for i in range(2)])
            t2r3 = Ring([(sb("t2z%d" % i, [128, 256], F32, p3), T("t2z%d" % i)) for i in range(2)])
            ytr = Ring([(sb("ytok%d" % i, [128, 256], BF16, p3), T("ytok%d" % i)) for i in range(4)])
            ysr = Ring([(sb("ystg%d" % i, [128, 2, 512], BF16, p3), T("ystg%d" % i)) for i in range(2)])
            qscale = 128.0 ** -0.5

            for h in range(4):
                for g in range(NG):
                    tok = slice(g * 512, (g + 1) * 512)
                    sc.op("pe", lambda e, h=h, tok=tok: e.matmul(pb[0][:, :], lhsT=wg[:, 0, h * 128:(h + 1) * 128],
                                                                 rhs=lrT[:, tok], start=True, stop=True),
                          reads=[t_wg, t_lrT[g]], writes=[pbt[0]])
                    sc.op("pe", lambda e, h=h, tok=tok: e.matmul(pb[1][:, :], lhsT=wg[:, 1, h * 128:(h + 1) * 128],
                                                                 rhs=lrT[:, tok], start=True, stop=True),
                          reads=[t_wg, t_lrT[g]], writes=[pbt[1]])
                    sc.op("act", lambda e, h=h: e.activation(out=spf[:], in_=pb[0][:, :], func=AF.Exp,
                                                             bias=nbias[:, h:h + 1], scale=-1.0),
                          reads=[pbt[0], t_nbias], writes=[t_spf])
                    sc.op("act", lambda e: e.activation(out=spf[:], in_=spf[:], func=AF.Ln, bias=1.0, scale=1.0),
                          reads=[t_spf], writes=[t_spf])
                    sc.op("act", lambda e, h=h: e.activation(out=spb[:], in_=pb[1][:, :], func=AF.Exp,
                                                             bias=nbias[:, 4 + h:5 + h], scale=-1.0),
                          reads=[pbt[1], t_nbias], writes=[t_spb])
                    sc.op("act", lambda e: e.activation(out=spb[:], in_=spb[:], func=AF.Ln, bias=1.0, scale=1.0),
                          reads=[t_spb], writes=[t_spb])
                    sc.op("dve", lambda e: e.tensor_tensor_scan(out=cumf[:], data0=rm_f[:], data1=spf[:], initial=0.0,
                                                                op0=ALU.mult, op1=ALU.add),
                          reads=[t_rm, t_spf], writes=[t_cumf])
                    sc.op("dve", lambda e: e.tensor_tensor_scan(out=cumb[:, ::-1], data0=rm_b[:, ::-1], data1=spb[:, ::-1],
                                                                initial=0.0, op0=ALU.mult, op1=ALU.add),
                          reads=[t_rm, t_spb], writes=[t_cumb])
                    sc.op("act", lambda e: e.activation(out=Ef[:], in_=cumf[:], func=AF.Exp, scale=-1.0 / 16),
                          reads=[t_cumf], writes=[t_Ef])
                    sc.op("act", lambda e: e.activation(out=Eif[:], in_=cumf[:], func=AF.Exp, scale=1.0 / 16),
                          reads=[t_cumf], writes=[t_Eif])
                    sc.op("act", lambda e: e.activation(out=Eb[:], in_=cumb[:], func=AF.Exp, scale=-1.0 / 16),
                          reads=[t_cumb], writes=[t_Eb])
                    sc.op("act", lambda e: e.activation(out=Eib[:], in_=cumb[:], func=AF.Exp, scale=1.0 / 16),
                          reads=[t_cumb], writes=[t_Eib])
                    sc.op("pool", lambda e, g=g: e.tensor_copy(out=Dall[:, 0, 4 * g:4 * g + 4], in_=Ef[:, 127:512:128]),
                          reads=[t_Ef], writes=[t_D[g]])
                    sc.op("pool", lambda e, g=g: e.tensor_copy(out=Dall[:, 1, 4 * g:4 * g + 4], in_=Eb[:, 0:512:128]),
                          reads=[t_Eb], writes=[t_D[g]])
                    for (w_, t_w, bank) in ((wq, t_wq, 2), (wk, t_wk, 3)):
                        for k in range(KC):
                            sc.op("pe", lambda e, w_=w_, bank=bank, k=k, tok=tok: e.matmul(
                                pb[bank][:, :], lhsT=w_[:, k, :], rhs=hT[:, k, tok], start=(k == 0), stop=(k == KC - 1)),
                                reads=[t_w] + hT_g[g], writes=[pbt[bank]], signal=(k == KC - 1))
                    sc.op("dve", lambda e, tok=tok: e.scalar_tensor_tensor(out=qf[:, tok], in0=pb[2][:, :], scalar=qscale,
                                                                           in1=Ef[:], op0=ALU.mult, op1=ALU.mult),
                          reads=[pbt[2], t_Ef], writes=[t_qk[g]])
                    sc.op("dve", lambda e, tok=tok: e.tensor_tensor(out=kf[:, tok], in0=pb[3][:, :], in1=Eif[:], op=ALU.mult),
                          reads=[pbt[3], t_Eif], writes=[t_qk[g]])
                    sc.op("dve", lambda e, tok=tok: e.scalar_tensor_tensor(out=qb[:, tok], in0=pb[2][:, :], scalar=qscale,
                                                                           in1=Eb[:], op0=ALU.mult, op1=ALU.mult),
                          reads=[pbt[2], t_Eb], writes=[t_qk[g]])
                    sc.op("dve", lambda e, tok=tok: e.tensor_tensor(out=kb[:, tok], in0=pb[3][:, :], in1=Eib[:], op=ALU.mult),
                          reads=[pbt[3], t_Eib], writes=[t_qk[g]])
                def B_pair(tp):
                    bank = 6 + (tp % 2)
                    for j in range(2):
                        t = 2 * tp + j
                        for k in range(KC):
                            sc.op("pe", lambda e, j=j, t=t, k=k: e.matmul(
                                pb[bank][:, j * 256:(j + 1) * 256], lhsT=hT[:, k, t * 128:(t + 1) * 128], rhs=wv[:, k, :],
                                start=(k == 0), stop=(k == KC - 1)),
                                reads=[t_wv, t_hT[t]], writes=[pbt[bank]], signal=(k == KC - 1))
                    sc.op("act", lambda e: e.activation(
                        out=v_h[:, 2 * tp:2 * tp + 2, :], in_=pb[bank][:, :].rearrange("p (j v) -> p j v", j=2), func=AF.Copy),
                        reads=[pbt[bank]], writes=[t_v[tp]])

                def K_scale(n, src, dirn):
                    ch = slice(n * 128, (n + 1) * 128)
                    g = n // 4
                    kout, t_ko = kor.next()
                    sc.op("act", lambda e: e.activation(out=kout[:], in_=src[:, ch], func=AF.Copy,
                                                        scale=Dall[:, dirn, n:n + 1]),
                          reads=[t_qk[g], t_D[g]], writes=[t_ko])
                    return kout, t_ko

                def K_tr(kout, t_ko, pview, pT):
                    ktok, t_kt = ktr.next()
                    sc.op("pe", lambda e: e.transpose(pview, kout[:], ident[:]),
                          reads=[t_ko, t_ident], writes=[pT])
                    sc.op("dve", lambda e: e.tensor_copy(out=ktok[:], in_=pview),
                          reads=[pT], writes=[t_kt])
                    return ktok, t_kt

                def K_stage(n, src, dirn, pview, pT, eng="act"):
                    kout, t_ko = K_scale(n, src, dirn)
                    return K_tr(kout, t_ko, pview, pT)

                s_old, t_so = s32r.next()
                sc.op("pool", lambda e, s_old=s_old: e.memset(s_old[:], 0.0), writes=[t_so])
                sc.op("pool", lambda e: e.memset(sbs[:, NT - 1, :], 0.0), writes=[t_sbs[NT - 1]])
                LC = 3
                kt_q = {}
                NP = NT // 2
                B_pair(NP - 1)
                for m in range(NT - 1, max(NT - 1 - LC, 0), -1):
                    kt_q[m] = K_stage(m, kb, 1, pbh[m % 2][:, 0:128], pbt[m % 2])
                pend_copy = []
                for tp in range(NP - 1, -1, -1):
                    ko_of = {}
                    for n in (2 * tp + 1, 2 * tp):
                        if n >= 1 and n - LC >= 1:
                            ko_of[n - LC] = K_scale(n - LC, kb, 1)
                    while pend_copy:
                        pend_copy.pop()()
                    if tp - 1 >= 0:
                        B_pair(tp - 1)
                    for n in (2 * tp + 1, 2 * tp):
                        if n < 1:
                            continue
                        g = n // 4
                        bs = 2 + (n % 4)
                        ktok, t_kt = kt_q.pop(n)
                        if n - LC >= 1:
                            ko_ = ko_of.pop(n - LC)
                            kt_q[n - LC] = K_tr(ko_[0], ko_[1], pbh[(n - LC) % 2][:, 0:128], pbt[(n - LC) % 2])
                        s_new, t_sn = s32r.next()
                        sc.op("pe", lambda e, bs=bs, ktok=ktok, n=n: e.matmul(pb[bs][:, 0:256], lhsT=ktok[:], rhs=v_h[:, n, :],
                                                                              start=True, stop=True),
                              reads=[t_kt, t_v[n // 2]], writes=[pbt[bs]])
                        sc.op("dve", lambda e, bs=bs, s_old=s_old, s_new=s_new, n=n: e.scalar_tensor_tensor(
                            out=s_new[:], in0=s_old[:], scalar=Dall[:, 1, n:n + 1], in1=pb[bs][:, 0:256],
                            op0=ALU.mult, op1=ALU.add),
                            reads=[t_so, t_D[g], pbt[bs]], writes=[t_sn])
                        if n % 2 == 0:
                            pend_copy.append(lambda s_new=s_new, n=n, t_sn=t_sn: sc.op(
                                "act", lambda e: e.activation(out=sbs[:, n - 1, :], in_=s_new[:], func=AF.Copy),
                                reads=[t_sn], writes=[t_sbs[n - 1]]))
                        else:
                            sc.op("pool", lambda e, s_new=s_new, n=n: e.tensor_copy(out=sbs[:, n - 1, :], in_=s_new[:]),
                                  reads=[t_sn], writes=[t_sbs[n - 1]])
                        s_old, t_so = s_new, t_sn
                while pend_copy:
                    pend_copy.pop()()
                s_old, t_so = s32r.next()
                sc.op("pool", lambda e, s_old=s_old: e.memset(s_old[:], 0.0), writes=[t_so])
                sfbf, t_sf = sfr.next()
                sc.op("pool", lambda e, sfbf=sfbf: e.memset(sfbf[:], 0.0), writes=[t_sf])
                if h + 1 < 4:
                    load_qkv(h + 1)
                att_of = {}
                z_of = {}
                y_of = {}

                def A_stage(n):
                    ch = slice(n * 128, (n + 1) * 128)
                    g = n // 4
                    ba = 0
                    attm, t_at = atr.next()
                    sc.op("pe", lambda e: e.matmul(pb[ba][:, 0:128], lhsT=kf[:, ch], rhs=qf[:, ch], start=True, stop=True),
                          reads=[t_qk[g]], writes=[pbt[ba]], signal=False)
                    sc.op("pe", lambda e: e.matmul(pb[ba][:, 128:256], lhsT=kb[:, ch], rhs=qb[:, ch], start=True, stop=True),
                          reads=[t_qk[g]], writes=[pbt[ba]])
                    sc.op("dve", lambda e: e.tensor_tensor(out=attm[:], in0=pb[ba][:, 0:256], in1=mask[:], op=ALU.mult),
                          reads=[pbt[ba], t_mask], writes=[t_at])
                    att_of[n] = (attm, t_at)

                def ZO_stage(n, sfbf, t_sf):
                    ch = slice(n * 128, (n + 1) * 128)
                    g = n // 4
                    bz_ = 3 + (n % 2)
                    bo = 1 + (n % 2)
                    for k in range(KC):
                        sc.op("pe", lambda e, k=k: e.matmul(pb[bz_][:, 0:256], lhsT=hT[:, k, ch], rhs=wz[:, k, :],
                                                            start=(k == 0), stop=(k == KC - 1)),
                              reads=[t_wz, t_hT[n]], writes=[pbt[bz_]], signal=(k == KC - 1))
                    attm, t_at = att_of.pop(n)
                    sc.op("pe", lambda e: e.matmul(pb[bo][:, 0:256], lhsT=attm[:, 0:128], rhs=v_h[:, n, :], start=True, stop=False),
                          reads=[t_at, t_v[n // 2]], writes=[pbt[bo]], signal=False)
                    sc.op("pe", lambda e: e.matmul(pb[bo][:, 0:256], lhsT=attm[:, 128:256], rhs=v_h[:, n, :], start=False, stop=False),
                          reads=[t_at, t_v[n // 2]], writes=[pbt[bo]], signal=False)
                    sc.op("pe", lambda e: e.matmul(pb[bo][:, 0:256], lhsT=qb[:, ch], rhs=sbs[:, n, :], start=False, stop=False),
                          reads=[t_qk[g], t_sbs[n]], writes=[pbt[bo]], signal=False)
                    sc.op("pe", lambda e: e.matmul(pb[bo][:, 0:256], lhsT=qf[:, ch], rhs=sfbf[:], start=False, stop=True),
                          reads=[t_qk[g], t_sf], writes=[pbt[bo]])
                    sg, t_sg = sgr3.next()
                    t2, t_t2 = t2r3.next()
                    t1, t_t1 = t1r3.next()
                    ytok, t_yt = ytr.next()
                    ss_, ln_, rstd, ts = new_stat()
                    sc.op("act", lambda e: e.activation(out=sg[:], in_=pb[bz_][:, 0:256], func=AF.Exp, scale=-1.0),
                          reads=[pbt[bz_]], writes=[t_sg])
                    sc.op("act", lambda e: e.activation(out=junk[:, 0:256], in_=pb[bo][:, 0:256], func=AF.Square, accum_out=ss_),
                          reads=[pbt[bo]], writes=[t_junk, ts])
                    sc.op("act", lambda e: e.activation(out=sg[:], in_=sg[:], func=AF.Ln, bias=1.0, scale=1.0),
                          reads=[t_sg], writes=[t_sg])
                    sc.op("act", lambda e: e.activation(out=ln_, in_=ss_, func=AF.Ln, bias=EPS, scale=1.0 / 256),
                          reads=[ts], writes=[ts])
                    sc.op("act", lambda e: e.activation(out=sg[:], in_=sg[:], func=AF.Exp, scale=-1.0),
                          reads=[t_sg], writes=[t_sg])
                    sc.op("act", lambda e: e.activation(out=rstd, in_=ln_, func=AF.Exp, scale=-0.5),
                          reads=[ts], writes=[ts])
                    sc.op("dve", lambda e: e.tensor_tensor(out=t2[:], in0=pb[bz_][:, 0:256], in1=sg[:], op=ALU.mult),
                          reads=[pbt[bz_], t_sg], writes=[t_t2])
                    sc.op("dve", lambda e: e.scalar_tensor_tensor(out=t1[:], in0=pb[bo][:, 0:256], scalar=rstd, in1=gng[:],
                                                                  op0=ALU.mult, op1=ALU.mult),
                          reads=[pbt[bo], ts, t_gng], writes=[t_t1])
                    sc.op("pool", lambda e: e.tensor_tensor(out=ytok[:], in0=t1[:], in1=t2[:], op=ALU.mult),
                          reads=[t_t1, t_t2], writes=[t_yt])
                    y_of[n] = (ytok, t_yt)

                def S_stage(n, ktok, t_kt, s_old, t_so):
                    g = n // 4
                    s_new, t_sn = s32r.next()
                    sfn, t_sfn = sfr.next()
                    sc.op("pe", lambda e: e.matmul(pb[5][:, 0:256], lhsT=ktok[:], rhs=v_h[:, n, :], start=True, stop=True),
                          reads=[t_kt, t_v[n // 2]], writes=[pbt[5]])
                    sc.op("dve", lambda e: e.scalar_tensor_tensor(out=s_new[:], in0=s_old[:], scalar=Dall[:, 0, n:n + 1],
                                                                  in1=pb[5][:, 0:256], op0=ALU.mult, op1=ALU.add),
                          reads=[t_so, t_D[g], pbt[5]], writes=[t_sn])
                    sc.op("pool", lambda e: e.tensor_copy(out=sfn[:], in_=s_new[:]),
                          reads=[t_sn], writes=[t_sfn])
                    return s_new, t_sn, sfn, t_sfn

                ystate = {"stg": None}

                def Y_stage(n):
                    g = n // 4
                    ytok, t_yt = y_of.pop(n)
                    for j in range(2):
                        sc.op("pe", lambda e, j=j: e.transpose(pbh[7][:, 256 + j * 128:256 + (j + 1) * 128],
                                                               ytok[:, j * 128:(j + 1) * 128], ident[:]),
                              reads=[t_yt, t_ident], writes=[pbt[7]], signal=(j == 1))
                    if n % 4 == 0:
                        ystate["stg"] = ysr.next()
                    ystg, t_ys = ystate["stg"]
                    sc.op("dve", lambda e: e.tensor_copy(
                        out=ystg[:, :, (n % 4) * 128:(n % 4 + 1) * 128],
                        in_=pbh[7][:, 256:512].rearrange("p (j t) -> p j t", j=2)),
                        reads=[pbt[7]], writes=[t_ys])
                    if n % 4 == 3:
                        sc.dma("sp", yT_v[:, 2 * h:2 * h + 2, g * 512:(g + 1) * 512], ystg[:], reads=[t_ys], tsem=t_ys)

                LK = 3
                kq = {}
                sf_of = {0: (sfbf, t_sf)}
                for m in range(0, min(LK, NT - 1)):
                    kq[m] = K_stage(m, kf, 0, pbh[6][:, 0:128], pbt[6], eng="act")
                if NT > 1:
                    kt_cur = kq.pop(0)
                    s_old, t_so, sfn_, t_sfn_ = S_stage(0, kt_cur[0], kt_cur[1], s_old, t_so)
                    sf_of[1] = (sfn_, t_sfn_)
                A_stage(0)
                for n in range(NT):
                    ko_ = K_scale(n + LK, kf, 0) if n + LK <= NT - 2 else None
                    if n + 1 <= NT - 2:
                        kt_cur = kq.pop(n + 1)
                        s_old, t_so, sfn_, t_sfn_ = S_stage(n + 1, kt_cur[0], kt_cur[1], s_old, t_so)
                        sf_of[n + 2] = (sfn_, t_sfn_)
                    if n + 1 < NT:
                        A_stage(n + 1)
                    sfb_, t_sfb_ = sf_of.pop(n)
                    ZO_stage(n, sfb_, t_sfb_)
                    if n >= 2:
                        Y_stage(n - 2)
                    if ko_ is not None:
                        kq[n + LK] = K_tr(ko_[0], ko_[1], pbh[6][:, 0:128], pbt[6])
                if NT >= 2:
                    Y_stage(NT - 2)
                Y_stage(NT - 1)
                if h + 1 < 4:
                    load_z(h + 1)
            sc.wait_all("pool", [b[1] for b in ysr.bufs])
            sc.op("pool", lambda e: e.memset(dummy[:], 0.0), writes=[t_dummy])
            sc.barrier()

        t_yT = T("yT_dram")
        with ExitStack() as p4:
            wo = sb("wo", [128, 16, D], BF16, p4); t_wo = [T("wo%d" % i) for i in range(4)]
            yr = Ring([(sb("yt%d" % i, [128, 16, 512], BF16, p4), [T("yt%d_%d" % (i, j)) for j in range(4)]) for i in range(2)])
            xr4 = Ring([(sb("x4_%d" % i, [128, D], F32, p4), T("x4_%d" % i)) for i in range(3)])
            rr = Ring([(sb("r4_%d" % i, [128, D], F32, p4), T("r4_%d" % i)) for i in range(3)])
            outs = []
            yt, t_yt4 = None, None
            for t in range(NT):
                g = t // 4
                if t % 4 == 0:
                    yt, t_yt4 = yr.next()
                    for i in range(4):
                        if t == 0:
                            sc.dma("sp", wo[:, 4 * i:4 * i + 4, :], wob_v[:, 4 * i:4 * i + 4, :], reads=[t_wob[i]], writes=[t_wo[i]])
                        sc.dma("sp", yt[:, 4 * i:4 * i + 4, :], yT_v[:, 4 * i:4 * i + 4, g * 512:(g + 1) * 512], writes=[t_yt4[i]])
                xt, t_xt = xr4.next()
                r, t_r = rr.next()
                sc.dma("sp", xt[:], x_d[t * 128:(t + 1) * 128, :], writes=[t_xt])
                tt = t % 4
                for half in range(2):
                    bank = 2 * (t % 2) + half
                    for c in range(16):
                        sc.op("pe", lambda e, bank=bank, c=c, tt=tt, half=half, yt=yt: e.matmul(
                            pb[bank][:, :], lhsT=yt[:, c, tt * 128:(tt + 1) * 128], rhs=wo[:, c, half * 512:(half + 1) * 512],
                            start=(c == 0), stop=(c == 15)),
                            reads=[t_yt4[c // 4], t_wo[c // 4]], writes=[pbt[bank]], signal=(c == 15))
                    sc.op("dve", lambda e, bank=bank, half=half, r=r, xt=xt: e.tensor_tensor(
                        out=r[:, half * 512:(half + 1) * 512], in0=pb[bank][:, :], in1=xt[:, half * 512:(half + 1) * 512], op=ALU.add),
                        reads=[pbt[bank], t_xt], writes=[t_r])
                rstd, ts = rms_rstd(r[:], [t_r], D, junk[:])
                sc.op("dve", lambda e, r=r, rstd=rstd: e.scalar_tensor_tensor(out=r[:], in0=r[:], scalar=rstd, in1=fg[:],
                                                                              op0=ALU.mult, op1=ALU.mult),
                      reads=[t_r, ts, t_fg], writes=[t_r])
                sc.dma("pool", out_d[t * 128:(t + 1) * 128, :], r[:], reads=[t_r], tsem=t_r)
            sc.wait_all("pool", [b[1] for b in rr.bufs])
        sc.emit()
    return nc


_NC_CACHE = {}


def kernel(x, norm_g, w_in, w_gk_f, b_gk_f, w_gk_b, b_gk_b, gla_norm_g, conv_w, conv_b, w_out, final_g):
    x = np.asarray(x, dtype=np.float32)
    B, S, _ = x.shape
    if S not in _NC_CACHE:
        _NC_CACHE[S] = build(S)
    nc = _NC_CACHE[S]
    f = lambda a: np.ascontiguousarray(np.asarray(a, dtype=np.float32))
    shared = {
        "norm_g": f(norm_g)[0], "w_in": f(w_in)[0], "w_gk_f": f(w_gk_f)[0], "b_gk_f": f(b_gk_f)[0],
        "w_gk_b": f(w_gk_b)[0], "b_gk_b": f(b_gk_b)[0], "gla_norm_g": f(gla_norm_g)[0],
        "conv_w": f(conv_w)[0], "conv_b": f(conv_b)[0], "w_out": f(w_out)[0], "final_g": f(final_g),
    }
    in_maps = []
    for b in range(B):
        m = dict(shared)
        m["x"] = np.ascontiguousarray(x[b])
        in_maps.append(m)
    res = run_bass_kernel_spmd(nc, in_maps, core_ids=list(range(B)))
    return np.stack([np.asarray(r["out"], dtype=np.float32) for r in res.results], axis=0)
```

```python
import numpy as np
from contextlib import ExitStack
import concourse.bass as bass
import concourse.mybir as mybir
from concourse.bass_utils import run_bass_kernel_spmd

F32 = mybir.dt.float32
BF16 = mybir.dt.bfloat16
AF = mybir.ActivationFunctionType
ALU = mybir.AluOpType

D = 1024
KC = 8
IN_W = 7200
EPS = 1e-6
OQ, OK_, OV, OZA, OLR, OB, OC, OH, OZC = 0, 512, 1024, 2048, 3072, 3104, 4128, 5152, 6176


class T:
    __slots__ = ("name", "w", "r", "dsem", "dcount")

    def __init__(self, name):
        self.name = name
        self.w = None
        self.r = {}
        self.dsem = None
        self.dcount = None


class EngQ:
    def __init__(self, name):
        self.name = name
        self.ops = []
        self.sem = None
        self.count = 0
        self.waited = {}


class Sched:
    ENGS = ("pe", "act", "dve", "pool", "sp")

    def __init__(self, nc, stack):
        self.nc = nc
        self.stack = stack
        self.q = {n: EngQ(n) for n in self.ENGS}
        for n in self.ENGS:
            self.q[n].sem = stack.enter_context(nc.semaphore("sem_" + n))
        self.nsem = 0

    def _need(self, q, ev, waits):
        if ev is None:
            return
        sem, val = ev
        k = id(sem)
        if sem is q.sem and val > q.count:
            return
        if q.waited.get(k, 0) >= val:
            return
        q.waited[k] = val
        waits[k] = (sem, max(val, waits.get(k, (None, 0))[1]))

    def _deps(self, q, reads, writes):
        waits = {}
        for t in reads:
            self._need(q, t.w, waits)
        for t in writes:
            self._need(q, t.w, waits)
            for ev in t.r.values():
                self._need(q, ev, waits)
        return list(waits.values())

    def _mark(self, ev, reads, writes):
        k = id(ev[0])
        for t in reads:
            old = t.r.get(k)
            if old is None or old[1] < ev[1]:
                t.r[k] = ev
        for t in writes:
            t.w = ev
            t.r = {}

    def op(self, eng, fn, reads=(), writes=(), signal=True):
        q = self.q[eng]
        waits = self._deps(q, reads, writes)
        ev = (q.sem, q.count + 1)
        if signal:
            q.count += 1
        sem = q.sem

        def emit(e, fn=fn, waits=waits, signal=signal, sem=sem):
            for (s, v) in waits:
                e.wait_ge(s, v)
            ins = fn(e)
            if signal:
                ins.then_inc(sem, 1)
        q.ops.append(emit)
        self._mark(ev, reads, writes)

    def dma(self, eng, out, in_, reads=(), writes=(), tsem=None, **kw):
        q = self.q[eng]
        waits = self._deps(q, reads, writes)
        t = tsem if tsem is not None else (writes[0] if writes else reads[0])
        if t.dsem is None:
            t.dsem = {}
            t.dcount = {}
        if eng not in t.dsem:
            t.dsem[eng] = self.stack.enter_context(
                self.nc.semaphore("dsem_%d" % self.nsem))
            t.dcount[eng] = 0
            self.nsem += 1
        t.dcount[eng] += 16
        dsem = t.dsem[eng]
        ev = (dsem, t.dcount[eng])

        def emit(e, waits=waits, out=out, in_=in_, sem=dsem, kw=kw):
            for (s, v) in waits:
                e.wait_ge(s, v)
            e.dma_start(out=out, in_=in_, **kw).then_inc(sem, 16)
        q.ops.append(emit)
        self._mark(ev, reads, writes)
        return ev

    def wait_all(self, eng, tiles):
        q = self.q[eng]
        waits = self._deps(q, (), tiles)

        def emit(e, waits=waits):
            for (s, v) in waits:
                e.wait_ge(s, v)
        q.ops.append(emit)

    def barrier(self):
        evs = [(self.q[n].sem, self.q[n].count) for n in self.ENGS if self.q[n].count > 0]
        for n in self.ENGS:
            q = self.q[n]
            waits = {}
            for ev in evs:
                self._need(q, ev, waits)
            wl = list(waits.values())

            def emit(e, wl=wl):
                for (s, v) in wl:
                    e.wait_ge(s, v)
            q.ops.append(emit)

    def emit(self):
        with self.nc.Block() as block:
            @block.tensor
            def _(e):
                for f in self.q["pe"].ops:
                    f(e)

            @block.scalar
            def _(e):
                for f in self.q["act"].ops:
                    f(e)

            @block.vector
            def _(e):
                for f in self.q["dve"].ops:
                    f(e)

            @block.gpsimd
            def _(e):
                for f in self.q["pool"].ops:
                    f(e)

            @block.sync
            def _(e):
                for f in self.q["sp"].ops:
                    f(e)


class Ring:
    def __init__(self, bufs):
        self.bufs = bufs
        self.i = 0

    def next(self):
        b = self.bufs[self.i % len(self.bufs)]
        self.i += 1
        return b


def build(S=4096):
    NT = S // 128
    NG = S // 512
    nc = bass.Bass("TRN2", target_bir_lowering=False, dynamic_dma_scratch_size=12288)
    x_d = nc.dram_tensor("x", [S, D], F32, kind="ExternalInput").ap()
    norm_g_d = nc.dram_tensor("norm_g", [D], F32, kind="ExternalInput").ap()
    w_in_d = nc.dram_tensor("w_in", [D, IN_W], F32, kind="ExternalInput").ap()
    w_gk_f_d = nc.dram_tensor("w_gk_f", [16, 512], F32, kind="ExternalInput").ap()
    b_gk_f_d = nc.dram_tensor("b_gk_f", [512], F32, kind="ExternalInput").ap()
    w_gk_b_d = nc.dram_tensor("w_gk_b", [16, 512], F32, kind="ExternalInput").ap()
    b_gk_b_d = nc.dram_tensor("b_gk_b", [512], F32, kind="ExternalInput").ap()
    gng_d = nc.dram_tensor("gla_norm_g", [256], F32, kind="ExternalInput").ap()
    conv_w_d = nc.dram_tensor("conv_w", [3, D], F32, kind="ExternalInput").ap()
    conv_b_d = nc.dram_tensor("conv_b", [D], F32, kind="ExternalInput").ap()
    w_out_d = nc.dram_tensor("w_out", [2048, D], F32, kind="ExternalInput").ap()
    final_g_d = nc.dram_tensor("final_g", [D], F32, kind="ExternalInput").ap()
    out_d = nc.dram_tensor("out", [S, D], F32, kind="ExternalOutput").ap()
    yT_d = nc.dram_tensor("yT_scratch", [16, 128, S], BF16, kind="Internal").ap()
    wob_d = nc.dram_tensor("wout_bf16", [2048, D], BF16, kind="Internal").ap()

    w_in_v = w_in_d.rearrange("(k p) c -> p k c", p=128)
    w_out_v = w_out_d.rearrange("(c p) d -> p c d", p=128)
    yT_v = yT_d.rearrange("c p s -> p c s")
    wob_v = wob_d.rearrange("(c p) d -> p c d", p=128)

    with ExitStack() as st:
        sc = Sched(nc, st)

        def sb(name, shape, dt, stack=st):
            return stack.enter_context(nc.sbuf_tensor(name, shape, dt))

        pb = []
        pbt = []
        for i in range(8):
            pb.append(st.enter_context(nc.psum_tensor("pb%d" % i, [128, 512], F32)))
            pbt.append(T("pb%d" % i))
        pbh = [p.bitcast(BF16) for p in pb]

        hT = sb("hT", [128, KC, S], BF16)
        t_hT = [T("hT%d" % t) for t in range(NT)]
        ident = sb("ident", [128, 128], BF16); t_ident = T("ident")
        mask = sb("mask", [128, 256], F32); t_mask = T("mask")
        rm_f = sb("rm_f", [128, 512], F32); t_rm = T("rm")
        rm_b = sb("rm_b", [128, 512], F32)
        gcol = sb("gcol", [128, KC], F32); t_gcol = T("gcol")
        gexp = sb("gexp", [128, KC, 128], F32); t_gexp = T("gexp")
        gng = sb("gng", [128, 256], F32); t_gng = T("gng")
        fg = sb("fg", [128, D], F32); t_fg = T("fg")
        nbias = sb("nbias", [128, 8], F32); t_nbias = T("nbias")
        cw = sb("cw", [128, 3, KC], F32); t_cw = T("cw")
        cb = sb("cb", [128, KC], F32); t_cb = T("cb")
        wg = sb("wg", [32, 2, 512], BF16); t_wg = T("wg")
        lrT = sb("lrT", [32, S], BF16); t_lrT = [T("lrT%d" % g) for g in range(NG)]
        wlr = sb("wlr", [128, KC, 32], BF16); t_wlr = T("wlr")
        junk = sb("junk", [128, D], BF16); t_junk = T("junk")
        dummy = sb("dummy_t", [128, 8], F32); t_dummy = T("dummy")

        sc.op("pool", lambda e: e.memset(ident[:], 0.0), writes=[t_ident])
        sc.op("pool", lambda e: e.affine_select(out=ident[:], in_=ident[:], compare_op=ALU.not_equal,
                                                fill=1.0, base=0, pattern=[[-1, 128]], channel_multiplier=1),
              reads=[t_ident], writes=[t_ident])
        sc.op("pool", lambda e: e.memset(mask[:], 1.0), writes=[t_mask])
        sc.op("pool", lambda e: e.affine_select(out=mask[:, 0:128], in_=mask[:, 0:128], compare_op=ALU.is_ge,
                                                fill=0.0, base=0, pattern=[[1, 128]], channel_multiplier=-1),
              reads=[t_mask], writes=[t_mask])
        sc.op("pool", lambda e: e.affine_select(out=mask[:, 128:256], in_=mask[:, 128:256], compare_op=ALU.is_gt,
                                                fill=0.0, base=0, pattern=[[-1, 128]], channel_multiplier=1),
              reads=[t_mask], writes=[t_mask])
        sc.op("pool", lambda e: e.memset(rm_f[:], 1.0), writes=[t_rm])
        sc.op("pool", lambda e: e.memset(rm_b[:], 1.0), writes=[t_rm])
        sc.op("pool", lambda e: e.memset(rm_f[:, 0:512:128], 0.0), writes=[t_rm])
        sc.op("pool", lambda e: e.memset(rm_b[:, 127:512:128], 0.0), writes=[t_rm])
        sc.op("pool", lambda e: e.memset(wg[:], 0.0), writes=[t_wg])
        sc.op("pool", lambda e: e.memset(gexp[:], 1.0), writes=[t_gexp])

        sc.dma("sp", gcol[:], norm_g_d.rearrange("(c p) -> p c", p=128), writes=[t_gcol],
               allow_slow_non_contiguous=True)
        for c in range(KC):
            sc.op("dve", lambda e, c=c: e.tensor_scalar(out=gexp[:, c, :], in0=gexp[:, c, :],
                                                        scalar1=gcol[:, c:c + 1], scalar2=None, op0=ALU.mult),
                  reads=[t_gcol, t_gexp], writes=[t_gexp])

        def load_small_consts():
            sc.dma("pool", wg[0:16, 0, :], w_gk_f_d, writes=[t_wg])
            sc.dma("pool", wg[16:32, 1, :], w_gk_b_d, writes=[t_wg])
            sc.dma("pool", wlr[:], w_in_v[:, :, OLR:OLR + 32], writes=[t_wlr])
            sc.dma("pool", gng[:], gng_d.partition_broadcast(128), writes=[t_gng])
            sc.dma("pool", fg[:], final_g_d.partition_broadcast(128), writes=[t_fg])
            sc.dma("pool", nbias[:, 0:4], b_gk_f_d.rearrange("(h p) -> p h", p=128), writes=[t_nbias],
                   allow_slow_non_contiguous=True)
            sc.dma("pool", nbias[:, 4:8], b_gk_b_d.rearrange("(h p) -> p h", p=128), writes=[t_nbias],
                   allow_slow_non_contiguous=True)
            sc.dma("pool", cw[:], conv_w_d.rearrange("k (c p) -> p k c", p=128), writes=[t_cw],
                   allow_slow_non_contiguous=True)
            sc.dma("pool", cb[:], conv_b_d.rearrange("(c p) -> p c", p=128), writes=[t_cb],
                   allow_slow_non_contiguous=True)

        stat = sb("stat", [128, 64], F32)
        t_stat = [T("stat%d" % i) for i in range(16)]
        stat_i = [0]

        def new_stat():
            i = stat_i[0] % 16
            stat_i[0] += 1
            return (stat[:, 4 * i:4 * i + 1], stat[:, 4 * i + 1:4 * i + 2], stat[:, 4 * i + 2:4 * i + 3], t_stat[i])

        def rms_rstd(src_ap, src_ts, n_elems, junk_ap):
            ss, lnv, rstd, ts = new_stat()
            sc.op("act", lambda e: e.activation(out=junk_ap, in_=src_ap, func=AF.Square, accum_out=ss),
                  reads=src_ts, writes=[t_junk, ts])
            sc.op("act", lambda e: e.activation(out=lnv, in_=ss, func=AF.Ln, bias=EPS, scale=1.0 / n_elems),
                  reads=[ts], writes=[ts])
            sc.op("act", lambda e: e.activation(out=rstd, in_=lnv, func=AF.Exp, scale=-0.5),
                  reads=[ts], writes=[ts])
            return rstd, ts

        wq = sb("wq", [128, KC, 128], BF16); t_wq = T("wq")
        wk = sb("wk", [128, KC, 128], BF16); t_wk = T("wk")
        wv = sb("wv", [128, KC, 256], BF16); t_wv = T("wv")
        wz = sb("wz", [128, KC, 256], BF16); t_wz = T("wz")
        p12 = ExitStack()
        p12.__enter__()
        p1 = p12
        xr = Ring([(sb("xt%d" % i, [128, D], F32, p1), T("xt%d" % i)) for i in range(6)])
        xnr = Ring([(sb("xn%d" % i, [128, D], BF16, p1), T("xn%d" % i)) for i in range(3)])

        def p1_tile(t):
            xt, t_xt = xr.next()
            xn, t_xn = xnr.next()
            bank = t % 4
            sc.dma("sp", xt[:], x_d[t * 128:(t + 1) * 128, :], writes=[t_xt])
            rstd, ts = rms_rstd(xt[:], [t_xt], D, junk[:])
            XS = 384
            sc.op("act", lambda e: e.activation(out=xn[:, 0:XS], in_=xt[:, 0:XS], func=AF.Copy, scale=rstd),
                  reads=[t_xt, ts], writes=[t_xn])
            sc.op("dve", lambda e: e.tensor_scalar(out=xn[:, XS:D], in0=xt[:, XS:D], scalar1=rstd, scalar2=None, op0=ALU.mult),
                  reads=[t_xt, ts], writes=[t_xn])
            for c in range(KC):
                sc.op("pe", lambda e, c=c: e.transpose(
                    pbh[bank][:, c * 128:(c + 1) * 128], xn[:, c * 128:(c + 1) * 128], ident[:]),
                    reads=[t_xn, t_ident], writes=[pbt[bank]], signal=(c == KC - 1))
            sc.op("dve", lambda e: e.tensor_tensor(
                out=hT[:, :, t * 128:(t + 1) * 128],
                in0=pbh[bank][:, :].rearrange("p (c t) -> p c t", c=KC),
                in1=gexp[:], op=ALU.mult),
                reads=[pbt[bank], t_gexp], writes=[t_hT[t]])


        def load_qkv(h):
            sc.dma("pool", wq[:], w_in_v[:, :, OQ + h * 128:OQ + (h + 1) * 128], writes=[t_wq])
            sc.dma("pool", wk[:], w_in_v[:, :, OK_ + h * 128:OK_ + (h + 1) * 128], writes=[t_wk])
            sc.dma("pool", wv[:], w_in_v[:, :, OV + h * 256:OV + (h + 1) * 256], writes=[t_wv])

        def load_z(h):
            sc.dma("pool", wz[:], w_in_v[:, :, OZA + h * 256:OZA + (h + 1) * 256], writes=[t_wz])

        hT_g = [t_hT[4 * g:4 * g + 4] for g in range(NG)]

        if True:
            p2 = p12
            wcr = Ring([(sb("wc%d" % i, [128, KC, 4, 128], BF16, p2), [T("wc%d_%d" % (i, j)) for j in range(4)])
                        for i in range(2)])
            u_full = sb("u_full", [128, S + 2], F32, p2)
            t_u = [T("u%d" % g) for g in range(NG)]
            t_upad = T("upad")
            hcr = Ring([(sb("hcs%d" % i, [128, 512], F32, p2), T("hcs%d" % i)) for i in range(2)])
            sgr = Ring([(sb("sg%d" % i, [128, 512], F32, p2), T("sg%d" % i)) for i in range(2)])
            bzr = Ring([(sb("bz%d" % i, [128, 512], F32, p2), T("bz%d" % i)) for i in range(3)])
            t1r = Ring([(sb("t1_%d" % i, [128, 512], F32, p2), T("t1_%d" % i)) for i in range(2)])
            ycr = Ring([(sb("yc%d" % i, [128, S], BF16, p2), T("yc%d" % i)) for i in range(2)])
            sc.op("pool", lambda e: e.memset(u_full[:, 0:1], 0.0), writes=[t_upad])
            sc.op("pool", lambda e: e.memset(u_full[:, S + 1:S + 2], 0.0), writes=[t_upad])
            offs = [OB, OC, OH, OZC]
            t_wob = [T("wob%d" % i) for i in range(4)]

            def load_wc(c):
                wc, t_wc = wcr.next()
                for part in range(4):
                    sc.dma("pool", wc[:, :, part, :], w_in_v[:, :, offs[part] + c * 128: offs[part] + (c + 1) * 128],
                           writes=[t_wc[part]])
                return wc, t_wc

            wc_next = load_wc(0)
            load_small_consts()
            for c in range(KC):
                wc, t_wc = wc_next
                if c + 1 < KC:
                    wc_next = load_wc(c + 1)
                yc, t_yc = ycr.next()
                bz_of = {}

                def conv_group(g, c=c, yc=yc, t_yc=t_yc):
                    c0 = 1 + g * 512
                    t1, t_t1 = t1r.next()
                    bz, t_bz = bz_of.pop(g)
                    rd = [t_u[g], t_upad]
                    if g > 0:
                        rd.append(t_u[g - 1])
                    if g < NG - 1:
                        rd.append(t_u[g + 1])
                    tail_eng = "dve" if (c == KC - 1 and g >= NG - 2) else "pool"
                    sc.op(tail_eng, lambda e: e.tensor_scalar(out=t1[:], in0=u_full[:, c0:c0 + 512],
                                                              scalar1=cw[:, 1, c:c + 1], scalar2=cb[:, c:c + 1],
                                                              op0=ALU.mult, op1=ALU.add),
                          reads=rd + [t_cw, t_cb], writes=[t_t1])
                    sc.op("dve", lambda e: e.scalar_tensor_tensor(out=t1[:], in0=u_full[:, c0 - 1:c0 + 511],
                                                                  scalar=cw[:, 0, c:c + 1], in1=t1[:],
                                                                  op0=ALU.mult, op1=ALU.add),
                          reads=rd + [t_cw, t_t1], writes=[t_t1])
                    sc.op("dve", lambda e: e.scalar_tensor_tensor(out=t1[:], in0=u_full[:, c0 + 1:c0 + 513],
                                                                  scalar=cw[:, 2, c:c + 1], in1=t1[:],
                                                                  op0=ALU.mult, op1=ALU.add),
                          reads=rd + [t_cw, t_t1], writes=[t_t1])
                    sc.op(tail_eng, lambda e: e.tensor_tensor(out=yc[:, g * 512:(g + 1) * 512], in0=t1[:], in1=bz[:],
                                                              op=ALU.mult),
                          reads=[t_t1, t_bz], writes=[t_yc])

                if c == min(1, KC - 1):
                    sc.op("pool", lambda e: e.tensor_scalar(out=nbias[:], in0=nbias[:], scalar1=-1.0, scalar2=None,
                                                            op0=ALU.mult), reads=[t_nbias], writes=[t_nbias])
                if 1 <= c <= 4:
                    i = c - 1
                    sc.dma("pool", wob_d[512 * i:512 * (i + 1), :], w_out_d[512 * i:512 * (i + 1), :], writes=[t_wob[i]])
                if c == KC - 2:
                    load_qkv(0)
                if c == KC - 1:
                    load_z(0)
                    for g_ in range(NG):
                        for k in range(KC):
                            sc.op("pe", lambda e, k=k, g_=g_: e.matmul(pb[0][0:32, :], lhsT=wlr[:, k, :],
                                                                       rhs=hT[:, k, g_ * 512:(g_ + 1) * 512],
                                                                       start=(k == 0), stop=(k == KC - 1)),
                                  reads=[t_wlr] + hT_g[g_], writes=[pbt[0]], signal=(k == KC - 1))
                        sc.op("act", lambda e, g_=g_: e.activation(out=lrT[:, g_ * 512:(g_ + 1) * 512], in_=pb[0][0:32, :], func=AF.Copy),
                              reads=[pbt[0]], writes=[t_lrT[g_]])
                for g in range(NG):
                    if c == 0 and g == 0:
                        for t in range(0, 4):
                            p1_tile(t)
                    b0 = 4 if c == 0 else 4 * (g % 2)
                    for part in range(4):
                        for k in range(KC):
                            sc.op("pe", lambda e, part=part, k=k, b0=b0, g=g, wc=wc: e.matmul(
                                pb[b0 + part][:, :], lhsT=wc[:, k, part, :], rhs=hT[:, k, g * 512:(g + 1) * 512],
                                start=(k == 0), stop=(k == KC - 1)),
                                reads=[t_wc[part]] + hT_g[g], writes=[pbt[b0 + part]], signal=(k == KC - 1))
                        if c == 0 and g + 1 < NG:
                            p1_tile(4 * (g + 1) + part)
                    p_b, p_c, p_h, p_z = (pb[b0 + i] for i in range(4))
                    tb, tcg, th, tz = (pbt[b0 + i] for i in range(4))
                    sg, t_sg = sgr.next()
                    hcs, t_hcs = hcr.next()
                    bz, t_bz = bzr.next()
                    bz_of[g] = (bz, t_bz)
                    sc.op("act", lambda e, sg=sg, p_z=p_z: e.activation(out=sg[:], in_=p_z[:, :], func=AF.Exp, scale=-1.0),
                          reads=[tz], writes=[t_sg])
                    sc.op("act", lambda e, sg=sg: e.activation(out=sg[:], in_=sg[:], func=AF.Ln, bias=1.0, scale=1.0),
                          reads=[t_sg], writes=[t_sg])
                    sc.op("act", lambda e, sg=sg: e.activation(out=sg[:], in_=sg[:], func=AF.Exp, scale=-1.0),
                          reads=[t_sg], writes=[t_sg])
                    sc.op("act", lambda e, hcs=hcs, p_h=p_h: e.activation(out=hcs[:], in_=p_h[:, :], func=AF.Copy),
                          reads=[th], writes=[t_hcs])
                    sc.op("dve", lambda e, bz=bz, p_b=p_b, sg=sg: e.tensor_tensor(out=bz[:], in0=p_b[:, :], in1=sg[:], op=ALU.mult),
                          reads=[tb, t_sg], writes=[t_bz])
                    sc.op("dve", lambda e, bz=bz, p_z=p_z: e.tensor_tensor(out=bz[:], in0=p_z[:, :], in1=bz[:], op=ALU.mult),
                          reads=[tz, t_bz], writes=[t_bz])
                    sc.op("dve", lambda e, g=g, p_c=p_c, hcs=hcs: e.tensor_tensor(
                        out=u_full[:, 1 + g * 512:1 + (g + 1) * 512], in0=p_c[:, :], in1=hcs[:], op=ALU.mult),
                        reads=[tcg, t_hcs], writes=[t_u[g]])
                    if g > 0:
                        conv_group(g - 1)
                conv_group(NG - 1)
                sc.dma("sp", yT_v[:, 8 + c, :], yc[:], reads=[t_yc], tsem=t_yc)
            sc.wait_all("pool", [b[1] for b in ycr.bufs])
            sc.op("pool", lambda e: e.memset(dummy[:], 0.0), writes=[t_dummy])
            sc.barrier()
        p12.__exit__(None, None, None)

        with ExitStack() as p3:
            qf = sb("qf", [128, S], BF16, p3); qb = sb("qb", [128, S], BF16, p3)
            kf = sb("kf", [128, S], BF16, p3); kb = sb("kb", [128, S], BF16, p3)
            t_qk = [T("qk%d" % g) for g in range(NG)]
            v_h = sb("v_h", [128, NT, 256], BF16, p3)
            t_v = [T("v%d" % (i)) for i in range(NT // 2)]
            sbs = sb("sbs", [128, NT, 256], BF16, p3)
            t_sbs = [T("sbs%d" % i) for i in range(NT)]
            spf = sb("spf", [128, 512], F32, p3); t_spf = T("spf")
            spb = sb("spb", [128, 512], F32, p3); t_spb = T("spb")
            cumf = sb("cumf", [128, 512], F32, p3); t_cumf = T("cumf")
            cumb = sb("cumb", [128, 512], F32, p3); t_cumb = T("cumb")
            Ef = sb("Ef", [128, 512], F32, p3); t_Ef = T("Ef")
            Eb = sb("Eb", [128, 512], F32, p3); t_Eb = T("Eb")
            Eif = sb("Eif", [128, 512], F32, p3); t_Eif = T("Eif")
            Eib = sb("Eib", [128, 512], F32, p3); t_Eib = T("Eib")
            Dall = sb("Dall", [128, 2, NT], F32, p3); t_D = [T("D%d" % g) for g in range(NG)]
            s32r = Ring([(sb("S32_%d" % i, [128, 256], F32, p3), T("S32_%d" % i)) for i in range(3)])
            sfr = Ring([(sb("sfbf%d" % i, [128, 256], BF16, p3), T("sfbf%d" % i)) for i in range(4)])
            ktr = Ring([(sb("ktok%d" % i, [128, 128], BF16, p3), T("ktok%d" % i)) for i in range(6)])
            kor = Ring([(sb("kout%d" % i, [128, 128], BF16, p3), T("kout%d" % i)) for i in range(5)])
            atr = Ring([(sb("attm%d" % i, [128, 256], BF16, p3), T("attm%d" % i)) for i in range(3)])
            pbt7a = T("pb7a"); pbt7b = T("pb7b")
            sgr3 = Ring([(sb("sgz%d" % i, [128, 256], F32, p3), T("sgz%d" % i)) for i in range(2)])
            t1r3 = Ring([(sb("t1z%d" % i, [128, 256], F32, p3), T("t1z%d" % i)) for i in range(2)])
            t2r3 = Ring([(sb("t2z%d" % i, [128, 256], F32, p3), T("t2z%d" % i)) for i in range(2)])
            ytr = Ring([(sb("ytok%d" % i, [128, 256], BF16, p3), T("ytok%d" % i)) for i in range(4)])
            ysr = Ring([(sb("ystg%d" % i, [128, 2, 512], BF16, p3), T("ystg%d" % i)) for i in range(2)])
            qscale = 128.0 ** -0.5

            for h in range(4):
                for g in range(NG):
                    tok = slice(g * 512, (g + 1) * 512)
                    sc.op("pe", lambda e, h=h, tok=tok: e.matmul(pb[0][:, :], lhsT=wg[:, 0, h * 128:(h + 1) * 128],
                                                                 rhs=lrT[:, tok], start=True, stop=True),
                          reads=[t_wg, t_lrT[g]], writes=[pbt[0]])
                    sc.op("pe", lambda e, h=h, tok=tok: e.matmul(pb[1][:, :], lhsT=wg[:, 1, h * 128:(h + 1) * 128],
                                                                 rhs=lrT[:, tok], start=True, stop=True),
                          reads=[t_wg, t_lrT[g]], writes=[pbt[1]])
                    sc.op("act", lambda e, h=h: e.activation(out=spf[:], in_=pb[0][:, :], func=AF.Exp,
                                                             bias=nbias[:, h:h + 1], scale=-1.0),
                          reads=[pbt[0], t_nbias], writes=[t_spf])
                    sc.op("act", lambda e: e.activation(out=spf[:], in_=spf[:], func=AF.Ln, bias=1.0, scale=1.0),
                          reads=[t_spf], writes=[t_spf])
                    sc.op("act", lambda e, h=h: e.activation(out=spb[:], in_=pb[1][:, :], func=AF.Exp,
                                                             bias=nbias[:, 4 + h:5 + h], scale=-1.0),
                          reads=[pbt[1], t_nbias], writes=[t_spb])
                    sc.op("act", lambda e: e.activation(out=spb[:], in_=spb[:], func=AF.Ln, bias=1.0, scale=1.0),
                          reads=[t_spb], writes=[t_spb])
                    sc.op("dve", lambda e: e.tensor_tensor_scan(out=cumf[:], data0=rm_f[:], data1=spf[:], initial=0.0,
                                                                op0=ALU.mult, op1=ALU.add),
                          reads=[t_rm, t_spf], writes=[t_cumf])
                    sc.op("dve", lambda e: e.tensor_tensor_scan(out=cumb[:, ::-1], data0=rm_b[:, ::-1], data1=spb[:, ::-1],
                                                                initial=0.0, op0=ALU.mult, op1=ALU.add),
                          reads=[t_rm, t_spb], writes=[t_cumb])
                    sc.op("act", lambda e: e.activation(out=Ef[:], in_=cumf[:], func=AF.Exp, scale=-1.0 / 16),
                          reads=[t_cumf], writes=[t_Ef])
                    sc.op("act", lambda e: e.activation(out=Eif[:], in_=cumf[:], func=AF.Exp, scale=1.0 / 16),
                          reads=[t_cumf], writes=[t_Eif])
                    sc.op("act", lambda e: e.activation(out=Eb[:], in_=cumb[:], func=AF.Exp, scale=-1.0 / 16),
                          reads=[t_cumb], writes=[t_Eb])
                    sc.op("act", lambda e: e.activation(out=Eib[:], in_=cumb[:], func=AF.Exp, scale=1.0 / 16),
                          reads=[t_cumb], writes=[t_Eib])
                    sc.op("pool", lambda e, g=g: e.tensor_copy(out=Dall[:, 0, 4 * g:4 * g + 4], in_=Ef[:, 127:512:128]),
                          reads=[t_Ef], writes=[t_D[g]])
                    sc.op("pool", lambda e, g=g: e.tensor_copy(out=Dall[:, 1, 4 * g:4 * g + 4], in_=Eb[:, 0:512:128]),
                          reads=[t_Eb], writes=[t_D[g]])
                    for (w_, t_w, bank) in ((wq, t_wq, 2), (wk, t_wk, 3)):
                        for k in range(KC):
                            sc.op("pe", lambda e, w_=w_, bank=bank, k=k, tok=tok: e.matmul(
                                pb[bank][:, :], lhsT=w_[:, k, :], rhs=hT[:, k, tok], start=(k == 0), stop=(k == KC - 1)),
                                reads=[t_w] + hT_g[g], writes=[pbt[bank]], signal=(k == KC - 1))
                    sc.op("dve", lambda e, tok=tok: e.scalar_tensor_tensor(out=qf[:, tok], in0=pb[2][:, :], scalar=qscale,
                                                                           in1=Ef[:], op0=ALU.mult, op1=ALU.mult),
                          reads=[pbt[2], t_Ef], writes=[t_qk[g]])
                    sc.op("dve", lambda e, tok=tok: e.tensor_tensor(out=kf[:, tok], in0=pb[3][:, :], in1=Eif[:], op=ALU.mult),
                          reads=[pbt[3], t_Eif], writes=[t_qk[g]])
                    sc.op("dve", lambda e, tok=tok: e.scalar_tensor_tensor(out=qb[:, tok], in0=pb[2][:, :], scalar=qscale,
                                                                           in1=Eb[:], op0=ALU.mult, op1=ALU.mult),
                          reads=[pbt[2], t_Eb], writes=[t_qk[g]])
                    sc.op("dve", lambda e, tok=tok: e.tensor_tensor(out=kb[:, tok], in0=pb[3][:, :], in1=Eib[:], op=ALU.mult),
                          reads=[pbt[3], t_Eib], writes=[t_qk[g]])
                def B_pair(tp):
                    bank = 6 + (tp % 2)
                    for j in range(2):
                        t = 2 * tp + j
                        for k in range(KC):
                            sc.op("pe", lambda e, j=j, t=t, k=k: e.matmul(
                                pb[bank][:, j * 256:(j + 1) * 256], lhsT=hT[:, k, t * 128:(t + 1) * 128], rhs=wv[:, k, :],
                                start=(k == 0), stop=(k == KC - 1)),
                                reads=[t_wv, t_hT[t]], writes=[pbt[bank]], signal=(k == KC - 1))
                    sc.op("act", lambda e: e.activation(
                        out=v_h[:, 2 * tp:2 * tp + 2, :], in_=pb[bank][:, :].rearrange("p (j v) -> p j v", j=2), func=AF.Copy),
                        reads=[pbt[bank]], writes=[t_v[tp]])

                def K_scale(n, src, dirn):
                    ch = slice(n * 128, (n + 1) * 128)
                    g = n // 4
                    kout, t_ko = kor.next()
                    sc.op("act", lambda e: e.activation(out=kout[:], in_=src[:, ch], func=AF.Copy,
                                                        scale=Dall[:, dirn, n:n + 1]),
                          reads=[t_qk[g], t_D[g]], writes=[t_ko])
                    return kout, t_ko

                def K_tr(kout, t_ko, pview, pT):
                    ktok, t_kt = ktr.next()
                    sc.op("pe", lambda e: e.transpose(pview, kout[:], ident[:]),
                          reads=[t_ko, t_ident], writes=[pT])
                    sc.op("dve", lambda e: e.tensor_copy(out=ktok[:], in_=pview),
                          reads=[pT], writes=[t_kt])
                    return ktok, t_kt

                def K_stage(n, src, dirn, pview, pT, eng="act"):
                    kout, t_ko = K_scale(n, src, dirn)
                    return K_tr(kout, t_ko, pview, pT)

                s_old, t_so = s32r.next()
                sc.op("pool", lambda e, s_old=s_old: e.memset(s_old[:], 0.0), writes=[t_so])
                sc.op("pool", lambda e: e.memset(sbs[:, NT - 1, :], 0.0), writes=[t_sbs[NT - 1]])
                LC = 3
                kt_q = {}
                NP = NT // 2
                B_pair(NP - 1)
                for m in range(NT - 1, max(NT - 1 - LC, 0), -1):
                    kt_q[m] = K_stage(m, kb, 1, pbh[m % 2][:, 0:128], pbt[m % 2])
                pend_copy = []
                for tp in range(NP - 1, -1, -1):
                    ko_of = {}
                    for n in (2 * tp + 1, 2 * tp):
                        if n >= 1 and n - LC >= 1:
                            ko_of[n - LC] = K_scale(n - LC, kb, 1)
                    while pend_copy:
                        pend_copy.pop()()
                    if tp - 1 >= 0:
                        B_pair(tp - 1)
                    for n in (2 * tp + 1, 2 * tp):
                        if n < 1:
                            continue
                        g = n // 4
                        bs = 2 + (n % 4)
                        ktok, t_kt = kt_q.pop(n)
                        if n - LC >= 1:
                            ko_ = ko_of.pop(n - LC)
                            kt_q[n - LC] = K_tr(ko_[0], ko_[1], pbh[(n - LC) % 2][:, 0:128], pbt[(n - LC) % 2])
                        s_new, t_sn = s32r.next()
                        sc.op("pe", lambda e, bs=bs, ktok=ktok, n=n: e.matmul(pb[bs][:, 0:256], lhsT=ktok[:], rhs=v_h[:, n, :],
                                                                              start=True, stop=True),
                              reads=[t_kt, t_v[n // 2]], writes=[pbt[bs]])
                        sc.op("dve", lambda e, bs=bs, s_old=s_old, s_new=s_new, n=n: e.scalar_tensor_tensor(
                            out=s_new[:], in0=s_old[:], scalar=Dall[:, 1, n:n + 1], in1=pb[bs][:, 0:256],
                            op0=ALU.mult, op1=ALU.add),
                            reads=[t_so, t_D[g], pbt[bs]], writes=[t_sn])
                        if n % 2 == 0:
                            pend_copy.append(lambda s_new=s_new, n=n, t_sn=t_sn: sc.op(
                                "act", lambda e: e.activation(out=sbs[:, n - 1, :], in_=s_new[:], func=AF.Copy),
                                reads=[t_sn], writes=[t_sbs[n - 1]]))
                        else:
                            sc.op("pool", lambda e, s_new=s_new, n=n: e.tensor_copy(out=sbs[:, n - 1, :], in_=s_new[:]),
                                  reads=[t_sn], writes=[t_sbs[n - 1]])
                        s_old, t_so = s_new, t_sn
                while pend_copy:
                    pend_copy.pop()()
                s_old, t_so = s32r.next()
                sc.op("pool", lambda e, s_old=s_old: e.memset(s_old[:], 0.0), writes=[t_so])
                sfbf, t_sf = sfr.next()
                sc.op("pool", lambda e, sfbf=sfbf: e.memset(sfbf[:], 0.0), writes=[t_sf])
                if h + 1 < 4:
                    load_qkv(h + 1)
                att_of = {}
                z_of = {}
                y_of = {}

                def A_stage(n):
                    ch = slice(n * 128, (n + 1) * 128)
                    g = n // 4
                    ba = 0
                    attm, t_at = atr.next()
                    sc.op("pe", lambda e: e.matmul(pb[ba][:, 0:128], lhsT=kf[:, ch], rhs=qf[:, ch], start=True, stop=True),
                          reads=[t_qk[g]], writes=[pbt[ba]], signal=False)
                    sc.op("pe", lambda e: e.matmul(pb[ba][:, 128:256], lhsT=kb[:, ch], rhs=qb[:, ch], start=True, stop=True),
                          reads=[t_qk[g]], writes=[pbt[ba]])
                    sc.op("dve", lambda e: e.tensor_tensor(out=attm[:], in0=pb[ba][:, 0:256], in1=mask[:], op=ALU.mult),
                          reads=[pbt[ba], t_mask], writes=[t_at])
                    att_of[n] = (attm, t_at)

                def ZO_stage(n, sfbf, t_sf):
                    ch = slice(n * 128, (n + 1) * 128)
                    g = n // 4
                    bz_ = 3 + (n % 2)
                    bo = 1 + (n % 2)
                    for k in range(KC):
                        sc.op("pe", lambda e, k=k: e.matmul(pb[bz_][:, 0:256], lhsT=hT[:, k, ch], rhs=wz[:, k, :],
                                                            start=(k == 0), stop=(k == KC - 1)),
                              reads=[t_wz, t_hT[n]], writes=[pbt[bz_]], signal=(k == KC - 1))
                    attm, t_at = att_of.pop(n)
                    sc.op("pe", lambda e: e.matmul(pb[bo][:, 0:256], lhsT=attm[:, 0:128], rhs=v_h[:, n, :], start=True, stop=False),
                          reads=[t_at, t_v[n // 2]], writes=[pbt[bo]], signal=False)
                    sc.op("pe", lambda e: e.matmul(pb[bo][:, 0:256], lhsT=attm[:, 128:256], rhs=v_h[:, n, :], start=False, stop=False),
                          reads=[t_at, t_v[n // 2]], writes=[pbt[bo]], signal=False)
                    sc.op("pe", lambda e: e.matmul(pb[bo][:, 0:256], lhsT=qb[:, ch], rhs=sbs[:, n, :], start=False, stop=False),
                          reads=[t_qk[g], t_sbs[n]], writes=[pbt[bo]], signal=False)
                    sc.op("pe", lambda e: e.matmul(pb[bo][:, 0:256], lhsT=qf[:, ch], rhs=sfbf[:], start=False, stop=True),
                          reads=[t_qk[g], t_sf], writes=[pbt[bo]])
                    sg, t_sg = sgr3.next()
                    t2, t_t2 = t2r3.next()
                    t1, t_t1 = t1r3.next()
                    ytok, t_yt = ytr.next()
                    ss_, ln_, rstd, ts = new_stat()
                    sc.op("act", lambda e: e.activation(out=sg[:], in_=pb[bz_][:, 0:256], func=AF.Exp, scale=-1.0),
                          reads=[pbt[bz_]], writes=[t_sg])
                    sc.op("act", lambda e: e.activation(out=junk[:, 0:256], in_=pb[bo][:, 0:256], func=AF.Square, accum_out=ss_),
                          reads=[pbt[bo]], writes=[t_junk, ts])
                    sc.op("act", lambda e: e.activation(out=sg[:], in_=sg[:], func=AF.Ln, bias=1.0, scale=1.0),
                          reads=[t_sg], writes=[t_sg])
                    sc.op("act", lambda e: e.activation(out=ln_, in_=ss_, func=AF.Ln, bias=EPS, scale=1.0 / 256),
                          reads=[ts], writes=[ts])
                    sc.op("act", lambda e: e.activation(out=sg[:], in_=sg[:], func=AF.Exp, scale=-1.0),
                          reads=[t_sg], writes=[t_sg])
                    sc.op("act", lambda e: e.activation(out=rstd, in_=ln_, func=AF.Exp, scale=-0.5),
                          reads=[ts], writes=[ts])
                    sc.op("dve", lambda e: e.tensor_tensor(out=t2[:], in0=pb[bz_][:, 0:256], in1=sg[:], op=ALU.mult),
                          reads=[pbt[bz_], t_sg], writes=[t_t2])
                    sc.op("dve", lambda e: e.scalar_tensor_tensor(out=t1[:], in0=pb[bo][:, 0:256], scalar=rstd, in1=gng[:],
                                                                  op0=ALU.mult, op1=ALU.mult),
                          reads=[pbt[bo], ts, t_gng], writes=[t_t1])
                    sc.op("pool", lambda e: e.tensor_tensor(out=ytok[:], in0=t1[:], in1=t2[:], op=ALU.mult),
                          reads=[t_t1, t_t2], writes=[t_yt])
                    y_of[n] = (ytok, t_yt)

                def S_stage(n, ktok, t_kt, s_old, t_so):
                    g = n // 4
                    s_new, t_sn = s32r.next()
                    sfn, t_sfn = sfr.next()
                    sc.op("pe", lambda e: e.matmul(pb[5][:, 0:256], lhsT=ktok[:], rhs=v_h[:, n, :], start=True, stop=True),
                          reads=[t_kt, t_v[n // 2]], writes=[pbt[5]])
                    sc.op("dve", lambda e: e.scalar_tensor_tensor(out=s_new[:], in0=s_old[:], scalar=Dall[:, 0, n:n + 1],
                                                                  in1=pb[5][:, 0:256], op0=ALU.mult, op1=ALU.add),
                          reads=[t_so, t_D[g], pbt[5]], writes=[t_sn])
                    sc.op("pool", lambda e: e.tensor_copy(out=sfn[:], in_=s_new[:]),
                          reads=[t_sn], writes=[t_sfn])
                    return s_new, t_sn, sfn, t_sfn

                ystate = {"stg": None}

                def Y_stage(n):
                    g = n // 4
                    ytok, t_yt = y_of.pop(n)
                    for j in range(2):
                        sc.op("pe", lambda e, j=j: e.transpose(pbh[7][:, 256 + j * 128:256 + (j + 1) * 128],
                                                               ytok[:, j * 128:(j + 1) * 128], ident[:]),
                              reads=[t_yt, t_ident], writes=[pbt[7]], signal=(j == 1))
                    if n % 4 == 0:
                        ystate["stg"] = ysr.next()
                    ystg, t_ys = ystate["stg"]
                    sc.op("dve", lambda e: e.tensor_copy(
                        out=ystg[:, :, (n % 4) * 128:(n % 4 + 1) * 128],
                        in_=pbh[7][:, 256:512].rearrange("p (j t) -> p j t", j=2)),
                        reads=[pbt[7]], writes=[t_ys])
                    if n % 4 == 3:
                        sc.dma("sp", yT_v[:, 2 * h:2 * h + 2, g * 512:(g + 1) * 512], ystg[:], reads=[t_ys], tsem=t_ys)

                LK = 3
                kq = {}
                sf_of = {0: (sfbf, t_sf)}
                for m in range(0, min(LK, NT - 1)):
                    kq[m] = K_stage(m, kf, 0, pbh[6][:, 0:128], pbt[6], eng="act")
                if NT > 1:
                    kt_cur = kq.pop(0)
                    s_old, t_so, sfn_, t_sfn_ = S_stage(0, kt_cur[0], kt_cur[1], s_old, t_so)
                    sf_of[1] = (sfn_, t_sfn_)
                A_stage(0)
                for n in range(NT):
                    ko_ = K_scale(n + LK, kf, 0) if n + LK <= NT - 2 else None
                    if n + 1 <= NT - 2:
                        kt_cur = kq.pop(n + 1)
                        s_old, t_so, sfn_, t_sfn_ = S_stage(n + 1, kt_cur[0], kt_cur[1], s_old, t_so)
                        sf_of[n + 2] = (sfn_, t_sfn_)
                    if n + 1 < NT:
                        A_stage(n + 1)
                    sfb_, t_sfb_ = sf_of.pop(n)
                    ZO_stage(n, sfb_, t_sfb_)
                    if n >= 2:
                        Y_stage(n - 2)
                    if ko_ is not None:
                        kq[n + LK] = K_tr(ko_[0], ko_[1], pbh[6][:, 0:128], pbt[6])
                if NT >= 2:
                    Y_stage(NT - 2)
                Y_stage(NT - 1)
                if h + 1 < 4:
                    load_z(h + 1)
            sc.wait_all("pool", [b[1] for b in ysr.bufs])
            sc.op("pool", lambda e: e.memset(dummy[:], 0.0), writes=[t_dummy])
            sc.barrier()

        t_yT = T("yT_dram")
        with ExitStack() as p4:
            wo = sb("wo", [128, 16, D], BF16, p4); t_wo = [T("wo%d" % i) for i in range(4)]
            yr = Ring([(sb("yt%d" % i, [128, 16, 512], BF16, p4), [T("yt%d_%d" % (i, j)) for j in range(4)]) for i in range(2)])
            xr4 = Ring([(sb("x4_%d" % i, [128, D], F32, p4), T("x4_%d" % i)) for i in range(3)])
            rr = Ring([(sb("r4_%d" % i, [128, D], F32, p4), T("r4_%d" % i)) for i in range(3)])
            outs = []
            yt, t_yt4 = None, None
            for t in range(NT):
                g = t // 4
                if t % 4 == 0:
                    yt, t_yt4 = yr.next()
                    for i in range(4):
                        if t == 0:
                            sc.dma("sp", wo[:, 4 * i:4 * i + 4, :], wob_v[:, 4 * i:4 * i + 4, :], reads=[t_wob[i]], writes=[t_wo[i]])
                        sc.dma("sp", yt[:, 4 * i:4 * i + 4, :], yT_v[:, 4 * i:4 * i + 4, g * 512:(g + 1) * 512], writes=[t_yt4[i]])
                xt, t_xt = xr4.next()
                r, t_r = rr.next()
                sc.dma("sp", xt[:], x_d[t * 128:(t + 1) * 128, :], writes=[t_xt])
                tt = t % 4
                for half in range(2):
                    bank = 2 * (t % 2) + half
                    for c in range(16):
                        sc.op("pe", lambda e, bank=bank, c=c, tt=tt, half=half, yt=yt: e.matmul(
                            pb[bank][:, :], lhsT=yt[:, c, tt * 128:(tt + 1) * 128], rhs=wo[:, c, half * 512:(half + 1) * 512],
                            start=(c == 0), stop=(c == 15)),
                            reads=[t_yt4[c // 4], t_wo[c // 4]], writes=[pbt[bank]], signal=(c == 15))
                    sc.op("dve", lambda e, bank=bank, half=half, r=r, xt=xt: e.tensor_tensor(
                        out=r[:, half * 512:(half + 1) * 512], in0=pb[bank][:, :], in1=xt[:, half * 512:(half + 1) * 512], op=ALU.add),
                        reads=[pbt[bank], t_xt], writes=[t_r])
                rstd, ts = rms_rstd(r[:], [t_r], D, junk[:])
                sc.op("dve", lambda e, r=r, rstd=rstd: e.scalar_tensor_tensor(out=r[:], in0=r[:], scalar=rstd, in1=fg[:],
                                                                              op0=ALU.mult, op1=ALU.mult),
                      reads=[t_r, ts, t_fg], writes=[t_r])
                sc.dma("pool", out_d[t * 128:(t + 1) * 128, :], r[:], reads=[t_r], tsem=t_r)
            sc.wait_all("pool", [b[1] for b in rr.bufs])
        sc.emit()
    return nc


_NC_CACHE = {}


def kernel(x, norm_g, w_in, w_gk_f, b_gk_f, w_gk_b, b_gk_b, gla_norm_g, conv_w, conv_b, w_out, final_g):
    x = np.asarray(x, dtype=np.float32)
    B, S, _ = x.shape
    if S not in _NC_CACHE:
        _NC_CACHE[S] = build(S)
    nc = _NC_CACHE[S]
    f = lambda a: np.ascontiguousarray(np.asarray(a, dtype=np.float32))
    shared = {
        "norm_g": f(norm_g)[0], "w_in": f(w_in)[0], "w_gk_f": f(w_gk_f)[0], "b_gk_f": f(b_gk_f)[0],
        "w_gk_b": f(w_gk_b)[0], "b_gk_b": f(b_gk_b)[0], "gla_norm_g": f(gla_norm_g)[0],
        "conv_w": f(conv_w)[0], "conv_b": f(conv_b)[0], "w_out": f(w_out)[0], "final_g": f(final_g),
    }
    in_maps = []
    for b in range(B):
        m = dict(shared)
        m["x"] = np.ascontiguousarray(x[b])
        in_maps.append(m)
    res = run_bass_kernel_spmd(nc, in_maps, core_ids=list(range(B)))
    return np.stack([np.asarray(r["out"], dtype=np.float32) for r in res.results], axis=0)
```
